# Optimizing a Trainium2 kernel written in Bass

```python
import math
import jax, jax.numpy as jnp
from jax import lax
import numpy as np

D_MODEL = 4096
BATCH = 2
SEQ = 8192
DEPTH = 4

CHUNK = 64
Q_BLOCK = 128
MEM_LEN = 256
EPS = 1e-6
ROPE_THETA = 10000.0

A_HEADS = 16
A_NOPE = 128
A_ROPE = 64
A_QK = A_NOPE + A_ROPE
A_V = 128
A_WIDTH = A_HEADS * A_V
Q_LORA = 1024
KV_LORA = 512
B_HEADS = 8
B_DIM = 128
B_WIDTH = B_HEADS * B_DIM
M_HEADS = 4
M_DIM = 256
M_WIDTH = M_HEADS * M_DIM
N_BRANCH = 3

IN_SPLITS = (Q_LORA, KV_LORA, A_ROPE, A_WIDTH,
             3 * B_WIDTH, B_WIDTH,
             M_WIDTH, M_WIDTH,
             N_BRANCH * D_MODEL)
IN_COLS = sum(IN_SPLITS)

kernel_name = "hybrid_mla_stickbreak_mem_gated"


def rms_norm(x, g):
    xf = x.astype(jnp.float32)
    y = xf * lax.rsqrt(jnp.mean(xf * xf, axis=-1, keepdims=True) + EPS)
    return (y * g.astype(jnp.float32)).astype(x.dtype)


def rope(x, positions):
    half = x.shape[-1] // 2
    freqs = ROPE_THETA ** (-jnp.arange(half, dtype=jnp.float32) / half)
    ang = positions.astype(jnp.float32)[..., None] * freqs
    cos = jnp.cos(ang)[:, :, None, :]
    sin = jnp.sin(ang)[:, :, None, :]
    x1 = x[..., :half].astype(jnp.float32)
    x2 = x[..., half:].astype(jnp.float32)
    out = jnp.concatenate([x1 * cos - x2 * sin, x2 * cos + x1 * sin], axis=-1)
    return out.astype(x.dtype)


def to_blocks(t):
    b, s, h, d = t.shape
    return t.reshape(b, s // Q_BLOCK, Q_BLOCK, h, d).transpose(1, 0, 3, 2, 4)


def from_blocks(t):
    n, b, h, qb, d = t.shape
    return t.transpose(1, 0, 3, 2, 4).reshape(b, n * qb, h, d)


def chunk_causal_softmax_attention(q, k, v, scale):
    s_len = k.shape[1]
    kh = k.transpose(0, 2, 1, 3)
    vh = v.transpose(0, 2, 1, 3)
    k_chunk = jnp.arange(s_len) // CHUNK
    starts = jnp.arange(s_len // Q_BLOCK) * Q_BLOCK

    def one_block(args):
        qblk, start = args
        q_chunk = (start + jnp.arange(Q_BLOCK)) // CHUNK
        sc = jnp.einsum('bhqd,bhkd->bhqk', qblk, kh,
                        preferred_element_type=jnp.float32) * scale
        sc = jnp.where(k_chunk[None, :] <= q_chunk[:, None], sc, -jnp.inf)
        p = jax.nn.softmax(sc, axis=-1)
        return jnp.einsum('bhqk,bhkd->bhqd', p.astype(vh.dtype), vh)

    return from_blocks(lax.map(one_block, (to_blocks(q), starts)))


def stick_breaking_attention(q, k, v):
    s_len = k.shape[1]
    scale = 1.0 / math.sqrt(q.shape[-1])
    kh = k.transpose(0, 2, 1, 3)
    vh = v.transpose(0, 2, 1, 3)
    k_idx = jnp.arange(s_len)
    starts = jnp.arange(s_len // Q_BLOCK) * Q_BLOCK

    def one_block(args):
        qblk, start = args
        q_idx = start + jnp.arange(Q_BLOCK)
        z = jnp.einsum('bhqd,bhkd->bhqk', qblk, kh,
                       preferred_element_type=jnp.float32) * scale
        before = k_idx[None, :] < q_idx[:, None]
        log_beta = jax.nn.log_sigmoid(z)
        log_fail = jnp.where(before, jax.nn.log_sigmoid(-z), 0.0)
        later_fail = lax.cumsum(log_fail, axis=3, reverse=True) - log_fail
        w = jnp.where(before, jnp.exp(log_beta + later_fail), 0.0)
        return jnp.einsum('bhqk,bhkd->bhqd', w.astype(vh.dtype), vh)

    return from_blocks(lax.map(one_block, (to_blocks(q), starts)))


def memory_cross_attention(q, k, v):
    scale = 1.0 / math.sqrt(q.shape[-1])
    sc = jnp.einsum('bshd,bmhd->bhsm', q, k, preferred_element_type=jnp.float32) * scale
    p = jax.nn.softmax(sc, axis=-1)
    return jnp.einsum('bhsm,bmhd->bshd', p.astype(v.dtype), v)


def hybrid_layer(x, mem, positions, g_pre, w_in, g_q_lat, w_uq, g_kv_lat, w_ukv,
                 g_qn_a, g_kn_a, w_pa, w_pb, g_mem, w_mkv, g_qn_m, g_kn_m, w_pm, w_out):
    b, s, _ = x.shape
    h = rms_norm(x, g_pre)
    proj = h @ w_in
    cuts = [int(c) for c in np.cumsum(IN_SPLITS)[:-1]]
    c_q, c_kv, k_r, gate_a, qkv_b, gate_b, q_m, gate_m, merge = jnp.split(proj, cuts, axis=-1)

    q_a = (rms_norm(c_q, g_q_lat) @ w_uq).reshape(b, s, A_HEADS, A_QK)
    kv_a = (rms_norm(c_kv, g_kv_lat) @ w_ukv).reshape(b, s, A_HEADS, A_NOPE + A_V)
    k_nope, v_a = kv_a[..., :A_NOPE], kv_a[..., A_NOPE:]
    k_rope_shared = jnp.broadcast_to(k_r[:, :, None, :], (b, s, A_HEADS, A_ROPE))
    k_a = jnp.concatenate([k_nope, k_rope_shared], axis=-1)
    q_a = rms_norm(q_a, g_qn_a)
    k_a = rms_norm(k_a, g_kn_a)
    q_a = jnp.concatenate([q_a[..., :A_NOPE], rope(q_a[..., A_NOPE:], positions)], axis=-1)
    k_a = jnp.concatenate([k_a[..., :A_NOPE], rope(k_a[..., A_NOPE:], positions)], axis=-1)
    o_a = chunk_causal_softmax_attention(q_a, k_a, v_a, 1.0 / math.sqrt(A_QK))
    y_a = (o_a.reshape(b, s, A_WIDTH) * jax.nn.silu(gate_a)) @ w_pa

    qkv = qkv_b.reshape(b, s, 3, B_HEADS, B_DIM)
    o_b = stick_breaking_attention(qkv[:, :, 0], qkv[:, :, 1], qkv[:, :, 2])
    y_b = (o_b.reshape(b, s, B_WIDTH) * jax.nn.silu(gate_b)) @ w_pb

    kv_m = (rms_norm(mem, g_mem) @ w_mkv).reshape(b, mem.shape[1], 2, M_HEADS, M_DIM)
    q_mh = rms_norm(q_m.reshape(b, s, M_HEADS, M_DIM), g_qn_m)
    k_mh = rms_norm(kv_m[:, :, 0], g_kn_m)
    o_m = memory_cross_attention(q_mh, k_mh, kv_m[:, :, 1])
    y_m = (o_m.reshape(b, s, M_WIDTH) * jax.nn.silu(gate_m)) @ w_pm

    r = jax.nn.sigmoid(merge.reshape(b, s, N_BRANCH, D_MODEL))
    mixed = r[:, :, 0] * y_a + r[:, :, 1] * y_b + r[:, :, 2] * y_m
    return x + mixed @ w_out


def setup_inputs(seed: int = 0) -> dict:
    key = jax.random.key(seed)
    ks = jax.random.split(key, 20)
    f32 = jnp.float32

    def w(k, shape, fan_in):
        return jax.random.normal(k, shape, f32) * (fan_in ** -0.5)

    def gain(k, shape):
        return 1.0 + 0.01 * jax.random.normal(k, shape, f32)

    L = DEPTH
    return {
        "x": jax.random.normal(ks[0], (BATCH, SEQ, D_MODEL), f32),
        "mem": jax.random.normal(ks[1], (BATCH, MEM_LEN, D_MODEL), f32),
        "positions": jnp.broadcast_to(jnp.arange(SEQ, dtype=jnp.int32), (BATCH, SEQ)),
        "g_pre": gain(ks[2], (L, D_MODEL)),
        "w_in": w(ks[3], (L, D_MODEL, IN_COLS), D_MODEL),
        "g_q_lat": gain(ks[4], (L, Q_LORA)),
        "w_uq": w(ks[5], (L, Q_LORA, A_HEADS * A_QK), Q_LORA),
        "g_kv_lat": gain(ks[6], (L, KV_LORA)),
        "w_ukv": w(ks[7], (L, KV_LORA, A_HEADS * (A_NOPE + A_V)), KV_LORA),
        "g_qn_a": gain(ks[8], (L, A_QK)),
        "g_kn_a": gain(ks[9], (L, A_QK)),
        "w_pa": w(ks[10], (L, A_WIDTH, D_MODEL), A_WIDTH),
        "w_pb": w(ks[11], (L, B_WIDTH, D_MODEL), B_WIDTH),
        "g_mem": gain(ks[12], (L, D_MODEL)),
        "w_mkv": w(ks[13], (L, D_MODEL, 2 * M_WIDTH), D_MODEL),
        "g_qn_m": gain(ks[14], (L, M_DIM)),
        "g_kn_m": gain(ks[15], (L, M_DIM)),
        "w_pm": w(ks[16], (L, M_WIDTH, D_MODEL), M_WIDTH),
        "w_out": w(ks[17], (L, D_MODEL, D_MODEL), D_MODEL),
    }


def reference(x, mem, positions, g_pre, w_in, g_q_lat, w_uq, g_kv_lat, w_ukv,
              g_qn_a, g_kn_a, w_pa, w_pb, g_mem, w_mkv, g_qn_m, g_kn_m, w_pm, w_out):
    for l in range(DEPTH):
        x = hybrid_layer(x, mem, positions, g_pre[l], w_in[l], g_q_lat[l], w_uq[l],
                         g_kv_lat[l], w_ukv[l], g_qn_a[l], g_kn_a[l], w_pa[l], w_pb[l],
                         g_mem[l], w_mkv[l], g_qn_m[l], g_kn_m[l], w_pm[l], w_out[l])
    return x
```

```python
import math
from contextlib import ExitStack

import numpy as np
import concourse.bass as bass
import concourse.mybir as mybir
from concourse.bass_utils import run_bass_kernel_spmd

F32 = mybir.dt.float32
BF16 = mybir.dt.bfloat16
I32 = mybir.dt.int32
AF = mybir.ActivationFunctionType
ALU = mybir.AluOpType

D = 4096
EPS = 1e-6
NCH_F = 165
C_CQ, C_CKV, C_KR, C_GA, C_GB, C_GM, C_QB, C_KB, C_QM, C_MG = 0, 8, 12, 13, 29, 37, 45, 53, 61, 69
NCONST = 4610
NG_COLS = 84
ENGS = ("pe", "act", "dve", "pool", "sp")


class Buf:
    __slots__ = ("lw", "rd")

    def __init__(self):
        self.lw = None
        self.rd = []


class Op:
    __slots__ = ("eng", "fn", "deps", "dma", "signal", "sem", "key", "val", "emitted")

    def __init__(self, eng, fn, dma):
        self.eng = eng
        self.fn = fn
        self.dma = dma
        self.deps = []
        self.signal = False
        self.sem = None
        self.key = None
        self.val = None
        self.emitted = False


class Prog:
    def __init__(self, nc, es):
        self.nc = nc
        self.cur = {e: [] for e in ENGS}
        self.esem = {e: es.enter_context(nc.semaphore("s_" + e)) for e in ("pe", "act", "dve", "pool")}
        self.ecnt = {e: 0 for e in self.esem}
        self.K = {"sp": 12, "pool": 6, "act": 4}
        self.dsem = {q: [es.enter_context(nc.semaphore("d_%s%d" % (q, i))) for i in range(k)]
                     for q, k in self.K.items()}
        self.dcnt = {q: 0 for q in self.K}
        self.dlast = {q: [None] * k for q, k in self.K.items()}
        self.waited = {e: {} for e in ENGS}
        self.lastop = {e: None for e in ENGS}
        self.nops = 0

    def op(self, eng, fn, reads=(), writes=(), dma=False):
        o = Op(eng, fn, dma)
        self.nops += 1
        deps = []
        for b in reads:
            if b.lw is not None:
                deps.append((b.lw, True))
        for b in writes:
            if b.lw is not None:
                deps.append((b.lw, False))
            for r in b.rd:
                deps.append((r, False))
        seen = set()
        for d, raw in deps:
            if d is o or id(d) in seen:
                continue
            if (not d.dma) and (not dma) and d.eng == eng:
                if eng == "pe" or not raw:
                    continue
            seen.add(id(d))
            o.deps.append(d)
            if not d.emitted:
                d.signal = True
        if dma:
            k = self.dcnt[eng]
            kk = self.K[eng]
            slot = k % kk
            o.sem = self.dsem[eng][slot]
            o.key = "d_%s%d" % (eng, slot)
            o.val = 16 * (k // kk + 1)
            prev = self.dlast[eng][slot]
            if prev is not None:
                o.deps.append(prev)
            self.dlast[eng][slot] = o
            self.dcnt[eng] = k + 1
        else:
            o.sem = self.esem[eng]
            o.key = "s_" + eng
        for b in reads:
            b.rd.append(o)
        for b in writes:
            b.lw = o
            b.rd = []
        self.cur[eng].append(o)
        if fn is not None:
            self.lastop[eng] = o
        return o

    def barrier(self):
        tg = [o for o in self.lastop.values() if o is not None and not o.dma]
        for q in self.K:
            tg += [o for o in self.dlast[q] if o is not None]
        for o in tg:
            if not o.emitted:
                o.signal = True
        for e in ENGS:
            b = Op(e, None, False)
            b.deps = list(tg)
            self.cur[e].append(b)

    def emit(self):
        for e in ("pe", "act", "dve", "pool"):
            ops = self.cur[e]
            real = [o for o in ops if o.fn is not None and not o.dma]
            if real:
                real[-1].signal = True
            c = self.ecnt[e]
            for o in real:
                if o.signal:
                    c += 1
                    o.val = c
            nxt = None
            for o in reversed(real):
                if o.val is None:
                    o.val = nxt
                else:
                    nxt = o.val
            self.ecnt[e] = c
        with self.nc.Block() as block:
            decos = {"pe": block.tensor, "act": block.scalar, "dve": block.vector,
                     "pool": block.gpsimd, "sp": block.sync}
            for e in ENGS:
                ops = self.cur[e]
                if not ops:
                    continue

                def body(eng, ops=ops, e=e):
                    w = self.waited[e]
                    for o in ops:
                        for d in o.deps:
                            assert d.val is not None
                            if w.get(d.key, 0) < d.val:
                                eng.wait_ge(d.sem, d.val)
                                w[d.key] = d.val
                        if o.fn is not None:
                            ins = o.fn(eng)
                            if o.dma:
                                ins.then_inc(o.sem, 16)
                            elif o.signal:
                                ins.then_inc(o.sem, 1)
                        o.emitted = True

                decos[e](body)
        self.cur = {e: [] for e in ENGS}


def build(S, DEPTH, debug=()):
    nc = bass.Bass("TRN2", target_bir_lowering=False)
    NT = S // 128
    NG = S // 512
    TG = 512

    def din(name, shape, dt=F32):
        return nc.dram_tensor(name, list(shape), dt, kind="ExternalInput")

    def dscr(name, shape, dt):
        kind = "ExternalOutput" if name in debug else "Internal"
        return nc.dram_tensor(name, list(shape), dt, kind=kind)

    x_in = din("x", [S, D])
    mem_in = din("mem", [256, D])
    pos_in = din("pos", [1, S], I32)
    consts_in = din("consts", [128, NCONST])
    gpack_in = din("gpack", [DEPTH, 128, NG_COLS])
    wF = din("wF", [DEPTH, NCH_F, 128, 32 * 128])
    wVB = din("wVB", [DEPTH, 4, 128, 32 * 256])
    wUQ = din("wUQ", [DEPTH, 32, 128, 8 * 128])
    wUK = din("wUK", [DEPTH, 16, 128, 4 * 128])
    wUV = din("wUV", [DEPTH, 8, 128, 4 * 256])
    wPA = din("wPA", [DEPTH, 32, 128, 16 * 128])
    wPB = din("wPB", [DEPTH, 32, 128, 8 * 128])
    wPM = din("wPM", [DEPTH, 32, 128, 8 * 128])
    wOUT = din("wOUT", [DEPTH, 16, 128, 32 * 256])
    wMK = din("wMK", [DEPTH, 8, 128, 32 * 128])
    wMV = din("wMV", [DEPTH, 4, 128, 32 * 256])
    y_out = nc.dram_tensor("y", [S, D], F32, kind="ExternalOutput")

    QTa = dscr("QTa", [16, 2, 128, S], BF16)
    KTa = dscr("KTa", [16, 2, 128, S], BF16)
    Va = dscr("Va", [S, 2048], BF16)
    QTb = dscr("QTb", [8, 128, S], BF16)
    KTb = dscr("KTb", [8, 128, S], BF16)
    Vb = dscr("Vb", [S, 1024], BF16)
    QTm = dscr("QTm", [8, 128, S], BF16)
    GT = dscr("GT", [32, 128, S], BF16)
    RT = dscr("RT", [96, 128, S], BF16)
    OGT = dscr("OGT", [32, 128, S], BF16)
    CSd = dscr("CSd", [128, S], F32)
    xs = [dscr("xs0", [S, D], F32), dscr("xs1", [S, D], F32)]

    b_CS = Buf()
    b_x = {}

    def bx(t, g):
        k = (t.name, g)
        if k not in b_x:
            b_x[k] = Buf()
        return b_x[k]

    b_A = [Buf() for _ in range(NG)]
    b_OG = [Buf() for _ in range(NG)]

    with ExitStack() as es:
        P = Prog(nc, es)
        sb = lambda name, shape, dt: es.enter_context(nc.sbuf_tensor(name, list(shape), dt))
        cb = sb("cb", [128, 4608], BF16)
        fcol = sb("fcol", [128, 2], F32)
        gp = sb("gp", [128, NG_COLS], F32)
        KmT = sb("KmT", [128, 8, 256], BF16)
        Vm = sb("Vm", [128, 2, 1024], BF16)
        b_cb, b_fcol, b_gp, b_KmT, b_Vm = Buf(), Buf(), Buf(), Buf(), Buf()
        ident = cb[:, 0:128]
        ones = cb[:, 128:256]
        triN = cb[:, 256:384]
        dupI = cb[:, 384:512]
        maskA = [cb[:, 512 + j * 512: 512 + (j + 1) * 512] for j in range(4)]
        maskB = [cb[:, 2560 + j * 512: 2560 + (j + 1) * 512] for j in range(4)]

        with ExitStack() as st:
            tb_ = lambda name, shape, dt: st.enter_context(nc.sbuf_tensor(name, list(shape), dt))
            cst = tb_("cst", [128, NCONST], F32)
            posi = tb_("posi", [128, S], I32)
            t0 = tb_("t0", [128, S], F32)
            t1 = tb_("t1", [128, S], F32)
            ki = tb_("ki", [128, S], I32)
            b_cst, b_posi, b_t0, b_t1, b_ki = Buf(), Buf(), Buf(), Buf(), Buf()
            P.op("sp", lambda e: e.dma_start(out=cst[:], in_=consts_in.ap()), (), (b_cst,), dma=True)
            P.op("sp", lambda e: e.dma_start(out=posi[:], in_=bass.AP(pos_in, 0, [[0, 128], [1, S]])),
                 (), (b_posi,), dma=True)
            P.op("dve", lambda e: e.tensor_copy(out=cb[:], in_=cst[:, 0:4608]), (b_cst,), (b_cb,))
            P.op("dve", lambda e: e.tensor_copy(out=fcol[:], in_=cst[:, 4608:4610]), (b_cst,), (b_fcol,))
            P.op("dve", lambda e: e.tensor_copy(out=t0[:], in_=posi[:]), (b_posi,), (b_t0,))
            P.op("dve", lambda e: e.tensor_scalar(out=t0[:], in0=t0[:], scalar1=fcol[:, 0:1], scalar2=None,
                                                  op0=ALU.mult), (b_t0, b_fcol), (b_t0,))
            P.op("dve", lambda e: e.tensor_scalar(out=t0[:], in0=t0[:], scalar1=float(1.0 / (2 * math.pi)),
                                                  scalar2=None, op0=ALU.mult), (b_t0,), (b_t0,))
            P.op("dve", lambda e: e.tensor_scalar(out=t0[:], in0=t0[:], scalar1=fcol[:, 1:2], scalar2=None,
                                                  op0=ALU.add), (b_t0, b_fcol), (b_t0,))
            P.op("dve", lambda e: e.tensor_copy(out=ki[:], in_=t0[:]), (b_t0,), (b_ki,))
            P.op("dve", lambda e: e.tensor_copy(out=t1[:], in_=ki[:]), (b_ki,), (b_t1,))
            P.op("dve", lambda e: e.tensor_tensor(out=t0[:], in0=t0[:], in1=t1[:], op=ALU.subtract),
                 (b_t0, b_t1), (b_t0,))
            P.op("dve", lambda e: e.scalar_tensor_tensor(out=t1[:], in0=t0[:], scalar=0.5, in1=t0[:],
                                                         op0=ALU.is_gt, op1=ALU.subtract), (b_t0,), (b_t1,))
            P.op("dve", lambda e: e.tensor_scalar(out=t0[:], in0=t1[:], scalar1=0.5, scalar2=None, op0=ALU.is_gt),
                 (b_t1,), (b_t0,))
            P.op("dve", lambda e: e.tensor_tensor(out=t1[:], in0=t1[:], in1=t0[:], op=ALU.subtract),
                 (b_t0, b_t1), (b_t1,))
            P.op("act", lambda e: e.activation(out=t0[:], in_=t1[:], func=AF.Sin, scale=float(-2 * math.pi)),
                 (b_t1,), (b_t0,))
            P.op("sp", lambda e: e.dma_start(out=CSd.ap(), in_=t0[:]), (b_t0,), (b_CS,), dma=True)
            P.barrier()
            P.emit()

        for l in range(DEPTH):
            x_cur = x_in if l == 0 else xs[(l - 1) % 2]
            x_nxt = y_out if l == DEPTH - 1 else xs[l % 2]
            stage_A(nc, P, l, S, locals())
            stage_B(nc, P, l, S, locals())
            stage_C(nc, P, l, S, locals())
        P.barrier()
        P.emit()
    return nc


def _rot(lst, state, key):
    i = state.get(key, 0)
    state[key] = i + 1
    return i % len(lst)


def stage_A(nc, P, l, S, env):
    g = env
    SFX = "_A%d" % l
    NG = S // 512
    cb, gp, fcol, KmT, Vm = g["cb"], g["gp"], g["fcol"], g["KmT"], g["Vm"]
    b_cb, b_gp, b_KmT, b_Vm = g["b_cb"], g["b_gp"], g["b_KmT"], g["b_Vm"]
    ident, ones, dupI = g["ident"], g["ones"], g["dupI"]
    x_cur = g["x_cur"]
    with ExitStack() as st:
        sbt = lambda name, shape, dt: st.enter_context(nc.sbuf_tensor(name + SFX, list(shape), dt))
        pst = lambda name, shape, dt: st.enter_context(nc.psum_tensor(name + SFX, list(shape), dt))
        hT = sbt("hT", [128, 32, 512], BF16)
        xt = sbt("xt", [128, D], F32)
        xn = sbt("xn", [128, D], BF16)
        wst = [sbt("wst%d" % i, [128, 4096], F32) for i in range(2)]
        wbf = [sbt("wbf%d" % i, [128, 4096], BF16) for i in range(2)]
        cq = sbt("cq", [128, 8, 512], BF16)
        ckv = sbt("ckv", [128, 4, 512], BF16)
        raw = [sbt("raw%d" % i, [128, 512], F32) for i in range(3)]
        sq = [sbt("sq%d" % i, [128, 512], BF16) for i in range(3)]
        so = [sbt("so%d" % i, [128, 512], BF16) for i in range(4)]
        rs = [sbt("rs%d" % i, [128, 512], F32) for i in range(2)]
        sskr = sbt("sskr", [128, 512], F32)
        kbase = sbt("kbase", [128, 512], F32)
        cst_ = sbt("cst_", [128, 512], F32)
        vst = [sbt("vst%d" % i, [128, 2048], BF16) for i in range(2)]
        sm = sbt("sm", [128, 8], F32)
        memT = hT
        acc = [pst("acc%d" % i, [128, 512], F32) for i in range(2)]
        ssb = [pst("ssb%d" % i, [128, 512], F32) for i in range(2)]
        trp = [pst("trp%d" % i, [128, 1024], BF16) for i in range(2)]
        tok = [pst("tok%d" % i, [128, 512], F32) for i in range(2)]
        B = lambda n: [Buf() for _ in range(n)]
        b_hT, b_xt, b_xn, b_cq, b_ckv, b_sskr, b_kbase, b_cst_, b_sm = (Buf() for _ in range(9))
        b_wst, b_wbf, b_raw, b_sq, b_so, b_rs, b_vst = B(2), B(2), B(3), B(3), B(4), B(2), B(2)
        b_acc, b_ssb, b_trp, b_tok = B(2), B(2), B(2), B(2)
        rot = {}

        P.barrier()
        P.op("sp", lambda e: e.dma_start(out=gp[:], in_=g["gpack_in"].ap()[l]), (), (b_gp,), dma=True)

        def wload(src, n):
            i = _rot(wst, rot, "w")
            P.op("sp", lambda e: e.dma_start(out=wst[i][:, :n], in_=src), (), (b_wst[i],), dma=True)
            ce = "pool" if (rot["w"] % 2 == 0) else "dve"
            P.op(ce, lambda e: e.tensor_copy(out=wbf[i][:, :n], in_=wst[i][:, :n]), (b_wst[i],), (b_wbf[i],))
            return wbf[i], b_wbf[i]

        def norm_T(src_rows, nblk, dstT, b_dst, gcol0, src_bufs=()):
            for tb in range(nblk):
                P.op("sp", lambda e, tb=tb: e.dma_start(out=xt[:], in_=src_rows(tb)), tuple(src_bufs), (b_xt,), dma=True)
                P.op("act", lambda e: e.activation(out=xn[:], in_=xt[:], func=AF.Square, accum_out=sm[:, 0:1]),
                     (b_xt,), (b_xn, b_sm))
                P.op("act", lambda e: e.activation(out=sm[:, 1:2], in_=sm[:, 0:1], func=AF.Sqrt,
                                                   scale=1.0 / D, bias=EPS), (b_sm,), (b_sm,))
                P.op("dve", lambda e: e.reciprocal(out=sm[:, 2:3], in_=sm[:, 1:2]), (b_sm,), (b_sm,))
                P.op("act", lambda e: e.activation(out=xn[:], in_=xt[:], func=AF.Copy, scale=sm[:, 2:3]),
                     (b_xt, b_sm), (b_xn,))
                for q in range(4):
                    j = _rot(trp, rot, "trp")
                    for u in range(8):
                        kc = q * 8 + u
                        P.op("pe", lambda e, kc=kc, u=u, j=j: e.transpose(
                            out=trp[j][:, u * 128:(u + 1) * 128], in_=xn[:, kc * 128:(kc + 1) * 128],
                            identity=ident), (b_xn, b_cb), (b_trp[j],))
                    for u in range(8):
                        kc = q * 8 + u
                        P.op("dve", lambda e, kc=kc, u=u, j=j, tb=tb: e.tensor_scalar(
                            out=dstT[:, kc, tb * 128:(tb + 1) * 128], in0=trp[j][:, u * 128:(u + 1) * 128],
                            scalar1=gp[:, gcol0 + kc:gcol0 + kc + 1], scalar2=None, op0=ALU.mult),
                            (b_trp[j], b_gp), (b_dst,))

        def fm_mm(wsrc, nk, rhs_of, rhs_bufs, ntok, a):
            w, bw = wload(wsrc, nk * 128)
            for kc in range(nk):
                P.op("pe", lambda e, kc=kc: e.matmul(acc[a][:, :ntok], w[:, kc * 128:(kc + 1) * 128], rhs_of(kc),
                                                     start=(kc == 0), stop=(kc == nk - 1)),
                     (bw,) + tuple(rhs_bufs), (b_acc[a],))

        def rstd_from(ssp, b_ssp, n, scale, bias, ntok):
            P.op("act", lambda e: e.activation(out=rs[n][:, :ntok], in_=ssp, func=AF.Sqrt, scale=scale, bias=bias),
                 (b_ssp,), (b_rs[n],))
            P.op("dve", lambda e: e.reciprocal(out=rs[n][:, :ntok], in_=rs[n][:, :ntok]), (b_rs[n],), (b_rs[n],))

        def store(dst_ap, src_ap, bsrc, bdst):
            P.op("sp", lambda e: e.dma_start(out=dst_ap, in_=src_ap), (bsrc,), (bdst,), dma=True)

        b_memT = b_hT
        norm_T(lambda tb: g["mem_in"].ap()[tb * 128:(tb + 1) * 128, :], 2, memT, b_memT, 48)
        rawm = sbt("rawm", [128, 8, 256], F32)
        b_rawm = Buf()
        for c in range(8):
            a = _rot(acc, rot, "acc")
            fm_mm(g["wMK"].ap()[l, c], 32, lambda kc: memT[:, kc, 0:256], (b_memT,), 256, a)
            P.op("act", lambda e, c=c, a=a: e.activation(out=rawm[:, c, :], in_=acc[a][:, :256], func=AF.Copy),
                 (b_acc[a],), (b_rawm,))
            i = _rot(sq, rot, "sq")
            P.op("dve", lambda e, c=c, a=a, i=i: e.tensor_tensor(out=sq[i][:, :256], in0=acc[a][:, :256],
                                                                 in1=rawm[:, c, :], op=ALU.mult),
                 (b_acc[a], b_rawm), (b_sq[i],))
            h, cc = c // 2, c % 2
            P.op("pe", lambda e, i=i, cc=cc: e.matmul(ssb[0][:, :256], ones, sq[i][:, :256], start=(cc == 0),
                                                      stop=(cc == 1)), (b_sq[i], b_cb), (b_ssb[0],))
            if cc == 1:
                rstd_from(ssb[0][:, :256], b_ssb[0], 0, 1.0 / 256, EPS, 256)
                for c2 in (c - 1, c):
                    P.op("dve", lambda e, c2=c2: e.scalar_tensor_tensor(
                        out=KmT[:, c2, :], in0=rawm[:, c2, :], scalar=gp[:, 82 + c2 % 2:83 + c2 % 2],
                        in1=rs[0][:, :256], op0=ALU.mult, op1=ALU.mult), (b_rawm, b_gp, b_rs[0]), (b_KmT,))
        for mb in range(2):
            for vc in range(4):
                t = _rot(tok, rot, "tok")
                for half in range(2):
                    w, bw = wload(g["wMV"].ap()[l, vc][:, half * 4096:(half + 1) * 4096], 4096)
                    for k2 in range(16):
                        kc = half * 16 + k2
                        P.op("pe", lambda e, kc=kc, k2=k2, t=t, mb=mb, w=w: e.matmul(
                            tok[t][:, :256], memT[:, kc, mb * 128:(mb + 1) * 128], w[:, k2 * 256:(k2 + 1) * 256],
                            start=(kc == 0), stop=(kc == 31)), (bw, b_memT), (b_tok[t],))
                P.op("act", lambda e, t=t, mb=mb, vc=vc: e.activation(
                    out=Vm[:, mb, vc * 256:(vc + 1) * 256], in_=tok[t][:, :256], func=AF.Copy),
                    (b_tok[t],), (b_Vm,))

        for tg in range(NG):
            t0_, t1_ = tg * 512, (tg + 1) * 512
            bA = g["b_A"][tg]
            norm_T(lambda tb, t0_=t0_: x_cur.ap()[t0_ + tb * 128: t0_ + (tb + 1) * 128, :], 4, hT, b_hT, 0, (g["bx"](x_cur, tg),))
            P.op("sp", lambda e, t0_=t0_, t1_=t1_: e.dma_start(out=cst_[:], in_=g["CSd"].ap()[:, t0_:t1_]), (g["b_CS"],), (b_cst_,),
                 dma=True)
            hrhs = lambda kc: hT[:, kc, :]

            for (c0, n, dst, b_dst, gc, ssi) in ((C_CQ, 8, cq, b_cq, 32, 0), (C_CKV, 4, ckv, b_ckv, 40, 1)):
                for c in range(n):
                    a = _rot(acc, rot, "acc")
                    fm_mm(g["wF"].ap()[l, c0 + c], 32, hrhs, (b_hT,), 512, a)
                    r = _rot(raw, rot, "raw")
                    P.op("act", lambda e, a=a, r=r: e.activation(out=raw[r][:], in_=acc[a][:], func=AF.Copy),
                         (b_acc[a],), (b_raw[r],))
                    i = _rot(sq, rot, "sq")
                    P.op("dve", lambda e, a=a, r=r, i=i: e.tensor_tensor(out=sq[i][:], in0=acc[a][:], in1=raw[r][:],
                                                                         op=ALU.mult),
                         (b_acc[a], b_raw[r]), (b_sq[i],))
                    P.op("pool", lambda e, r=r, c=c, dst=dst: e.tensor_copy(out=dst[:, c, :], in_=raw[r][:]),
                         (b_raw[r],), (b_dst,))
                    P.op("pe", lambda e, i=i, c=c, n=n, ssi=ssi: e.matmul(ssb[ssi][:], ones, sq[i][:],
                                                                         start=(c == 0), stop=(c == n - 1)),
                         (b_sq[i], b_cb), (b_ssb[ssi],))
                rstd_from(ssb[ssi][:], b_ssb[ssi], ssi, 1.0 / (n * 128), EPS, 512)
                for c in range(n):
                    P.op("dve", lambda e, c=c, dst=dst, gc=gc, ssi=ssi: e.scalar_tensor_tensor(
                        out=dst[:, c, :], in0=dst[:, c, :], scalar=gp[:, gc + c:gc + c + 1], in1=rs[ssi][:],
                        op0=ALU.mult, op1=ALU.mult), (b_dst, b_gp, b_rs[ssi]), (b_dst,))

            a = _rot(acc, rot, "acc")
            fm_mm(g["wF"].ap()[l, C_KR], 32, hrhs, (b_hT,), 512, a)
            r = _rot(raw, rot, "raw")
            P.op("act", lambda e, a=a, r=r: e.activation(out=raw[r][:], in_=acc[a][:], func=AF.Copy),
                 (b_acc[a],), (b_raw[r],))
            i = _rot(sq, rot, "sq")
            P.op("dve", lambda e, a=a, r=r, i=i: e.tensor_tensor(out=sq[i][:], in0=acc[a][:], in1=raw[r][:],
                                                                 op=ALU.mult), (b_acc[a], b_raw[r]), (b_sq[i],))
            P.op("pe", lambda e, i=i: e.matmul(ssb[0][:], ones[0:64, :], sq[i][0:64, :], start=True, stop=True),
                 (b_sq[i], b_cb), (b_ssb[0],))
            P.op("act", lambda e: e.activation(out=sskr[:], in_=ssb[0][:], func=AF.Copy), (b_ssb[0],), (b_sskr,))
            i2 = _rot(sq, rot, "sq")
            P.op("dve", lambda e, r=r, i2=i2: e.scalar_tensor_tensor(
                out=sq[i2][:], in0=raw[r][:], scalar=gp[:, 47:48], in1=cst_[:], op0=ALU.mult, op1=ALU.mult),
                (b_raw[r], b_gp, b_cst_), (b_sq[i2],))
            P.op("pe", lambda e, i2=i2: e.matmul(ssb[1][:], dupI, sq[i2][:], start=True, stop=True),
                 (b_sq[i2], b_cb), (b_ssb[1],))
            P.op("act", lambda e: e.activation(out=kbase[:], in_=ssb[1][:], func=AF.Copy), (b_ssb[1],), (b_kbase,))

            for h in range(16):
                an = _rot(acc, rot, "acc")
                fm_mm(g["wUQ"].ap()[l, h], 8, lambda kc: cq[:, kc, :], (b_cq,), 512, an)
                rn = _rot(raw, rot, "raw")
                P.op("act", lambda e, an=an, rn=rn: e.activation(out=raw[rn][:], in_=acc[an][:], func=AF.Copy),
                     (b_acc[an],), (b_raw[rn],))
                i = _rot(sq, rot, "sq")
                P.op("dve", lambda e, an=an, rn=rn, i=i: e.tensor_tensor(out=sq[i][:], in0=acc[an][:],
                                                                         in1=raw[rn][:], op=ALU.mult),
                     (b_acc[an], b_raw[rn]), (b_sq[i],))
                P.op("pe", lambda e, i=i: e.matmul(ssb[0][:], ones, sq[i][:], start=True, stop=False),
                     (b_sq[i], b_cb), (b_ssb[0],))
                ar = _rot(acc, rot, "acc")
                fm_mm(g["wUQ"].ap()[l, 16 + h], 8, lambda kc: cq[:, kc, :], (b_cq,), 512, ar)
                rr = _rot(raw, rot, "raw")
                P.op("act", lambda e, ar=ar, rr=rr: e.activation(out=raw[rr][:], in_=acc[ar][:], func=AF.Copy),
                     (b_acc[ar],), (b_raw[rr],))
                i = _rot(sq, rot, "sq")
                P.op("dve", lambda e, ar=ar, rr=rr, i=i: e.tensor_tensor(out=sq[i][:], in0=acc[ar][:],
                                                                         in1=raw[rr][:], op=ALU.mult),
                     (b_acc[ar], b_raw[rr]), (b_sq[i],))
                P.op("pe", lambda e, i=i: e.matmul(ssb[0][:], ones[0:64, :], sq[i][0:64, :], start=False, stop=True),
                     (b_sq[i], b_cb), (b_ssb[0],))
                rstd_from(ssb[0][:], b_ssb[0], 0, 1.0, 192 * EPS, 512)
                o1 = _rot(so, rot, "so")
                P.op("dve", lambda e, rn=rn, o1=o1: e.scalar_tensor_tensor(
                    out=so[o1][:], in0=raw[rn][:], scalar=gp[:, 44:45], in1=rs[0][:], op0=ALU.mult, op1=ALU.mult),
                    (b_raw[rn], b_gp, b_rs[0]), (b_so[o1],))
                store(g["QTa"].ap()[h, 0][:, t0_:t1_], so[o1][:], b_so[o1], bA)
                P.op("dve", lambda e, rr=rr: e.scalar_tensor_tensor(
                    out=raw[rr][:], in0=raw[rr][:], scalar=gp[:, 45:46], in1=rs[0][:], op0=ALU.mult, op1=ALU.mult),
                    (b_raw[rr], b_gp, b_rs[0]), (b_raw[rr],))
                o2 = _rot(so, rot, "so")
                P.op("pool", lambda e, rr=rr, o2=o2: e.tensor_tensor(out=so[o2][:], in0=raw[rr][:], in1=cst_[:],
                                                                     op=ALU.mult),
                     (b_raw[rr], b_cst_), (b_so[o2],))
                store(g["QTa"].ap()[h, 1][:, t0_:t1_], so[o2][:], b_so[o2], bA)

            for h in range(16):
                a = _rot(acc, rot, "acc")
                fm_mm(g["wUK"].ap()[l, h], 4, lambda kc: ckv[:, kc, :], (b_ckv,), 512, a)
                r = _rot(raw, rot, "raw")
                P.op("act", lambda e, a=a, r=r: e.activation(out=raw[r][:], in_=acc[a][:], func=AF.Copy),
                     (b_acc[a],), (b_raw[r],))
                i = _rot(sq, rot, "sq")
                P.op("dve", lambda e, a=a, r=r, i=i: e.tensor_tensor(out=sq[i][:], in0=acc[a][:], in1=raw[r][:],
                                                                     op=ALU.mult), (b_acc[a], b_raw[r]), (b_sq[i],))
                P.op("pe", lambda e, i=i: e.matmul(ssb[1][:], ones, sq[i][:], start=True, stop=True),
                     (b_sq[i], b_cb), (b_ssb[1],))
                P.op("dve", lambda e: e.tensor_tensor(out=rs[1][:], in0=ssb[1][:], in1=sskr[:], op=ALU.add),
                     (b_ssb[1], b_sskr), (b_rs[1],))
                rstd_from(rs[1][:], b_rs[1], 1, 1.0 / 192, EPS, 512)
                o1 = _rot(so, rot, "so")
                P.op("dve", lambda e, r=r, o1=o1: e.scalar_tensor_tensor(
                    out=so[o1][:], in0=raw[r][:], scalar=gp[:, 46:47], in1=rs[1][:], op0=ALU.mult, op1=ALU.mult),
                    (b_raw[r], b_gp, b_rs[1]), (b_so[o1],))
                store(g["KTa"].ap()[h, 0][:, t0_:t1_], so[o1][:], b_so[o1], bA)
                o2 = _rot(so, rot, "so")
                P.op("pool", lambda e, o2=o2: e.tensor_tensor(out=so[o2][:], in0=kbase[:], in1=rs[1][:], op=ALU.mult),
                     (b_kbase, b_rs[1]), (b_so[o2],))
                store(g["KTa"].ap()[h, 1][:, t0_:t1_], so[o2][:], b_so[o2], bA)

            for tb in range(4):
                v = _rot(vst, rot, "vst")
                for vc in range(8):
                    t = _rot(tok, rot, "tok")
                    w, bw = wload(g["wUV"].ap()[l, vc], 1024)
                    for kc in range(4):
                        P.op("pe", lambda e, kc=kc, t=t, tb=tb, w=w: e.matmul(
                            tok[t][:, :256], ckv[:, kc, tb * 128:(tb + 1) * 128], w[:, kc * 256:(kc + 1) * 256],
                            start=(kc == 0), stop=(kc == 3)), (bw, b_ckv), (b_tok[t],))
                    P.op("act", lambda e, t=t, v=v, vc=vc: e.activation(
                        out=vst[v][:, vc * 256:(vc + 1) * 256], in_=tok[t][:, :256], func=AF.Copy),
                        (b_tok[t],), (b_vst[v],))
                store(g["Va"].ap()[t0_ + tb * 128:t0_ + (tb + 1) * 128, :], vst[v][:], b_vst[v], bA)

            for tb in range(4):
                v = _rot(vst, rot, "vst")
                for vc in range(4):
                    t = _rot(tok, rot, "tok")
                    for half in range(2):
                        w, bw = wload(g["wVB"].ap()[l, vc][:, half * 4096:(half + 1) * 4096], 4096)
                        for k2 in range(16):
                            kc = half * 16 + k2
                            P.op("pe", lambda e, kc=kc, k2=k2, t=t, tb=tb, w=w: e.matmul(
                                tok[t][:, :256], hT[:, kc, tb * 128:(tb + 1) * 128], w[:, k2 * 256:(k2 + 1) * 256],
                                start=(kc == 0), stop=(kc == 31)), (bw, b_hT), (b_tok[t],))
                    P.op("act", lambda e, t=t, v=v, vc=vc: e.activation(
                        out=vst[v][:, vc * 256:(vc + 1) * 256], in_=tok[t][:, :256], func=AF.Copy),
                        (b_tok[t],), (b_vst[v],))
                store(g["Vb"].ap()[t0_ + tb * 128:t0_ + (tb + 1) * 128, :], vst[v][:, 0:1024], b_vst[v], bA)

            simple = []
            for c in range(32):
                simple.append((C_GA + c, AF.Silu, 1.0, GTd(g)[c]))
            for c in range(8):
                simple.append((C_QB + c, AF.Copy, 1.0 / math.sqrt(128.0), g["QTb"].ap()[c]))
            for c in range(8):
                simple.append((C_KB + c, AF.Copy, 1.0, g["KTb"].ap()[c]))
            for c in range(96):
                simple.append((C_MG + c, AF.Sigmoid, 1.0, g["RT"].ap()[c]))
            for (ci, fn_, sc, dst) in simple:
                a = _rot(acc, rot, "acc")
                fm_mm(g["wF"].ap()[l, ci], 32, hrhs, (b_hT,), 512, a)
                o1 = _rot(so, rot, "so")
                if fn_ == AF.Copy:
                    P.op("act", lambda e, a=a, o1=o1, sc=sc: e.mul(out=so[o1][:], in_=acc[a][:], mul=sc),
                         (b_acc[a],), (b_so[o1],))
                else:
                    P.op("act", lambda e, a=a, o1=o1, fn_=fn_: e.activation(out=so[o1][:], in_=acc[a][:], func=fn_),
                         (b_acc[a],), (b_so[o1],))
                store(dst[:, t0_:t1_], so[o1][:], b_so[o1], bA)

            for h in range(4):
                rr_ = []
                for cc in range(2):
                    a = _rot(acc, rot, "acc")
                    fm_mm(g["wF"].ap()[l, C_QM + 2 * h + cc], 32, hrhs, (b_hT,), 512, a)
                    r = _rot(raw, rot, "raw")
                    rr_.append(r)
                    P.op("act", lambda e, a=a, r=r: e.activation(out=raw[r][:], in_=acc[a][:], func=AF.Copy),
                         (b_acc[a],), (b_raw[r],))
                    i = _rot(sq, rot, "sq")
                    P.op("dve", lambda e, a=a, r=r, i=i: e.tensor_tensor(out=sq[i][:], in0=acc[a][:], in1=raw[r][:],
                                                                         op=ALU.mult),
                         (b_acc[a], b_raw[r]), (b_sq[i],))
                    P.op("pe", lambda e, i=i, cc=cc: e.matmul(ssb[0][:], ones, sq[i][:], start=(cc == 0),
                                                              stop=(cc == 1)), (b_sq[i], b_cb), (b_ssb[0],))
                rstd_from(ssb[0][:], b_ssb[0], 0, 1.0, 256 * EPS, 512)
                for cc in range(2):
                    o1 = _rot(so, rot, "so")
                    r = rr_[cc]
                    P.op("dve", lambda e, r=r, o1=o1, cc=cc: e.scalar_tensor_tensor(
                        out=so[o1][:], in0=raw[r][:], scalar=gp[:, 80 + cc:81 + cc], in1=rs[0][:],
                        op0=ALU.mult, op1=ALU.mult), (b_raw[r], b_gp, b_rs[0]), (b_so[o1],))
                    store(g["QTm"].ap()[2 * h + cc][:, t0_:t1_], so[o1][:], b_so[o1], bA)
        P.emit()


def GTd(g):
    return [g["GT"].ap()[c] for c in range(32)]


def stage_B(nc, P, l, S, env):
    g = env
    SFX = "_B%d" % l
    NT = S // 128
    NG = S // 512
    cb, KmT, Vm = g["cb"], g["KmT"], g["Vm"]
    b_cb, b_KmT, b_Vm = g["b_cb"], g["b_KmT"], g["b_Vm"]
    ones, triN = g["ones"], g["triN"]
    maskA, maskB = g["maskA"], g["maskB"]
    with ExitStack() as st:
        sbt = lambda name, shape, dt: st.enter_context(nc.sbuf_tensor(name + SFX, list(shape), dt))
        pst = lambda name, shape, dt: st.enter_context(nc.psum_tensor(name + SFX, list(shape), dt))
        KT = [sbt("KT%d" % i, [128, 2, S], BF16) for i in range(2)]
        VV = [sbt("VV%d" % i, [128, NT, 128], BF16) for i in range(2)]
        QT = [sbt("QT%d" % i, [128, 2, 512], BF16) for i in range(2)]
        GG = [sbt("GG%d" % i, [128, 512], BF16) for i in range(2)]
        PT = [sbt("PT%d" % i, [128, 512], BF16) for i in range(3)]
        LT = [sbt("LT%d" % i, [128, 512], BF16) for i in range(2)]
        EE = [sbt("EE%d" % i, [128, 512], F32) for i in range(2)]
        AR = [sbt("AR%d" % i, [128, 512], F32) for i in range(2)]
        Rsb = sbt("Rsb", [128, 512], F32)
        rden = sbt("rden", [128, 512], F32)
        of = sbt("of", [128, 512], F32)
        ob = [sbt("ob%d" % i, [128, 512], BF16) for i in range(2)]
        sps = [pst("sps%d" % i, [128, 512], F32) for i in range(3)]
        ops_ = [pst("ops%d" % i, [128, 512], F32) for i in range(2)]
        dps = [pst("dps%d" % i, [128, 512], F32) for i in range(2)]
        tps = pst("tps", [128, 512], F32)
        B = lambda n: [Buf() for _ in range(n)]
        b_KT, b_VV, b_QT, b_GG, b_PT, b_LT, b_EE, b_AR, b_ob = B(2), B(2), B(2), B(2), B(3), B(2), B(2), B(2), B(2)
        b_sps, b_ops, b_dps = B(3), B(2), B(2)
        b_Rsb, b_rden, b_of, b_tps = Buf(), Buf(), Buf(), Buf()
        rot = {}
        bAll = g["b_A"]
        P.barrier()

        def fin(osrc, b_osrc, den, gi, chunk, qg, norm):
            q0, q1 = qg * 512, (qg + 1) * 512
            if norm:
                P.op("dve", lambda e: e.reciprocal(out=rden[:], in_=den[0]), (den[1],), (b_rden,))
                P.op("dve", lambda e: e.tensor_tensor(out=of[:], in0=osrc, in1=rden[:], op=ALU.mult),
                     (b_osrc, b_rden), (b_of,))
                o = _rot(ob, rot, "ob")
                P.op("pool", lambda e, o=o: e.tensor_tensor(out=ob[o][:], in0=of[:], in1=GG[gi][:], op=ALU.mult),
                     (b_of, b_GG[gi]), (b_ob[o],))
            else:
                o = _rot(ob, rot, "ob")
                P.op("dve", lambda e, o=o: e.tensor_tensor(out=ob[o][:], in0=osrc, in1=GG[gi][:], op=ALU.mult),
                     (b_osrc, b_GG[gi]), (b_ob[o],))
            P.op("sp", lambda e, o=o: e.dma_start(out=g["OGT"].ap()[chunk][:, q0:q1], in_=ob[o][:]),
                 (b_ob[o],), (g["b_OG"][qg],), dma=True)

        for h in range(16):
            kb_ = _rot(KT, rot, "KT")
            P.op("sp", lambda e, kb_=kb_, h=h: e.dma_start(out=KT[kb_][:], in_=g["KTa"].ap()[h].rearrange("c p s -> p c s")),
                 tuple(bAll), (b_KT[kb_],), dma=True)
            for b0 in range(0, NT, 8):
                b1 = min(NT, b0 + 8)
                P.op("sp", lambda e, kb_=kb_, h=h, b0=b0, b1=b1: e.dma_start(
                    out=VV[kb_][:, b0:b1, :],
                    in_=g["Va"].ap()[b0 * 128:b1 * 128, h * 128:(h + 1) * 128].rearrange("(b p) d -> p b d", p=128)),
                    tuple(bAll), (b_VV[kb_],), dma=True)
            for qg in range(NG):
                q0, q1 = qg * 512, (qg + 1) * 512
                qi = _rot(QT, rot, "QT")
                P.op("sp", lambda e, qi=qi, h=h, q0=q0, q1=q1: e.dma_start(
                    out=QT[qi][:], in_=g["QTa"].ap()[h][:, :, q0:q1].rearrange("c p s -> p c s")),
                    (bAll[qg],), (b_QT[qi],), dma=True)
                gi = _rot(GG, rot, "GG")
                P.op("sp", lambda e, gi=gi, h=h, q0=q0, q1=q1: e.dma_start(out=GG[gi][:], in_=g["GT"].ap()[h][:, q0:q1]),
                     (bAll[qg],), (b_GG[gi],), dma=True)
                oi = _rot(ops_, rot, "ops")
                nkb = 4 * qg + 4
                for kb in range(nkb):
                    si = _rot(sps, rot, "sps")
                    for c in range(2):
                        P.op("pe", lambda e, si=si, c=c, kb=kb, kb_=kb_, qi=qi: e.matmul(
                            sps[si][:], KT[kb_][:, c, kb * 128:(kb + 1) * 128], QT[qi][:, c, :],
                            start=(c == 0), stop=(c == 1)), (b_KT[kb_], b_QT[qi]), (b_sps[si],))
                    pi = _rot(PT, rot, "PT")
                    P.op("act", lambda e, si=si, pi=pi: e.activation(out=PT[pi][:], in_=sps[si][:], func=AF.Exp),
                         (b_sps[si],), (b_PT[pi],))
                    j = kb - 4 * qg
                    if j >= 0:
                        P.op("pool", lambda e, pi=pi, j=j: e.tensor_tensor(out=PT[pi][:], in0=PT[pi][:], in1=maskA[j],
                                                                           op=ALU.mult),
                             (b_PT[pi], b_cb), (b_PT[pi],))
                    P.op("pe", lambda e, pi=pi, kb=kb, kb_=kb_, oi=oi, nkb=nkb: e.matmul(
                        ops_[oi][:], VV[kb_][:, kb, :], PT[pi][:], start=(kb == 0), stop=(kb == nkb - 1)),
                        (b_VV[kb_], b_PT[pi]), (b_ops[oi],))
                    P.op("pe", lambda e, pi=pi, kb=kb, oi=oi, nkb=nkb: e.matmul(
                        dps[oi][:], ones, PT[pi][:], start=(kb == 0), stop=(kb == nkb - 1)),
                        (b_cb, b_PT[pi]), (b_dps[oi],))
                fin(ops_[oi][:], b_ops[oi], (dps[oi][:], b_dps[oi]), gi, h, qg, True)

        for h in range(8):
            kb_ = _rot(KT, rot, "KT")
            P.op("sp", lambda e, kb_=kb_, h=h: e.dma_start(out=KT[kb_][:, 0, :], in_=g["KTb"].ap()[h]),
                 tuple(bAll), (b_KT[kb_],), dma=True)
            for b0 in range(0, NT, 8):
                b1 = min(NT, b0 + 8)
                P.op("sp", lambda e, kb_=kb_, h=h, b0=b0, b1=b1: e.dma_start(
                    out=VV[kb_][:, b0:b1, :],
                    in_=g["Vb"].ap()[b0 * 128:b1 * 128, h * 128:(h + 1) * 128].rearrange("(b p) d -> p b d", p=128)),
                    tuple(bAll), (b_VV[kb_],), dma=True)
            for qg in range(NG):
                q0, q1 = qg * 512, (qg + 1) * 512
                qi = _rot(QT, rot, "QT")
                P.op("sp", lambda e, qi=qi, h=h, q0=q0, q1=q1: e.dma_start(out=QT[qi][:, 0, :],
                                                                          in_=g["QTb"].ap()[h][:, q0:q1]),
                     (bAll[qg],), (b_QT[qi],), dma=True)
                gi = _rot(GG, rot, "GG")
                P.op("sp", lambda e, gi=gi, h=h, q0=q0, q1=q1: e.dma_start(out=GG[gi][:],
                                                                          in_=g["GT"].ap()[16 + h][:, q0:q1]),
                     (bAll[qg],), (b_GG[gi],), dma=True)
                oi = _rot(ops_, rot, "ops")
                nkb = 4 * qg + 4
                for n_, kb in enumerate(range(nkb - 1, -1, -1)):
                    j = kb - 4 * qg
                    si = _rot(sps, rot, "sps")
                    P.op("pe", lambda e, si=si, kb=kb, kb_=kb_, qi=qi: e.matmul(
                        sps[si][:], KT[kb_][:, 0, kb * 128:(kb + 1) * 128], QT[qi][:, 0, :], start=True, stop=False),
                        (b_KT[kb_], b_QT[qi]), (b_sps[si],))
                    ei = _rot(EE, rot, "EE")
                    P.op("act", lambda e, si=si, ei=ei: e.activation(out=EE[ei][:], in_=sps[si][:], func=AF.Exp),
                         (b_sps[si],), (b_EE[ei],))
                    li = _rot(LT, rot, "LT")
                    P.op("act", lambda e, ei=ei, li=li: e.activation(out=LT[li][:], in_=EE[ei][:], func=AF.Ln, bias=1.0),
                         (b_EE[ei],), (b_LT[li],))
                    if j >= 0:
                        P.op("pool", lambda e, li=li, j=j: e.tensor_tensor(out=LT[li][:], in0=LT[li][:], in1=maskB[j],
                                                                           op=ALU.mult),
                             (b_LT[li], b_cb), (b_LT[li],))
                    P.op("pe", lambda e, si=si, li=li: e.matmul(sps[si][:], triN, LT[li][:], start=False, stop=True,
                                                                skip_group_check=True),
                         (b_cb, b_LT[li]), (b_sps[si],))
                    ai = _rot(AR, rot, "AR")
                    if n_ == 0:
                        P.op("dve", lambda e, si=si, ai=ai: e.tensor_copy(out=AR[ai][:], in_=sps[si][:]),
                             (b_sps[si],), (b_AR[ai],))
                    else:
                        P.op("dve", lambda e, si=si, ai=ai: e.tensor_tensor(out=AR[ai][:], in0=sps[si][:], in1=Rsb[:],
                                                                            op=ALU.subtract),
                             (b_sps[si], b_Rsb), (b_AR[ai],))
                    pi = _rot(PT, rot, "PT")
                    P.op("act", lambda e, ai=ai, pi=pi: e.activation(out=PT[pi][:], in_=AR[ai][:], func=AF.Exp),
                         (b_AR[ai],), (b_PT[pi],))
                    if j >= 0:
                        P.op("pool", lambda e, pi=pi, j=j: e.tensor_tensor(out=PT[pi][:], in0=PT[pi][:], in1=maskB[j],
                                                                           op=ALU.mult),
                             (b_PT[pi], b_cb), (b_PT[pi],))
                    P.op("pe", lambda e, pi=pi, kb=kb, kb_=kb_, oi=oi, n_=n_, nkb=nkb: e.matmul(
                        ops_[oi][:], VV[kb_][:, kb, :], PT[pi][:], start=(n_ == 0), stop=(n_ == nkb - 1)),
                        (b_VV[kb_], b_PT[pi]), (b_ops[oi],))
                    if n_ < nkb - 1:
                        P.op("pe", lambda e, li=li: e.matmul(tps[:], ones, LT[li][:], start=True, stop=True),
                             (b_cb, b_LT[li]), (b_tps,))
                        if n_ == 0:
                            P.op("dve", lambda e: e.tensor_copy(out=Rsb[:], in_=tps[:]), (b_tps,), (b_Rsb,))
                        else:
                            P.op("dve", lambda e: e.tensor_tensor(out=Rsb[:], in0=tps[:], in1=Rsb[:], op=ALU.add),
                                 (b_tps, b_Rsb), (b_Rsb,))
                fin(ops_[oi][:], b_ops[oi], None, gi, 16 + h, qg, False)

        for qg in range(NG):
            q0, q1 = qg * 512, (qg + 1) * 512
            for h in range(4):
                qi = _rot(QT, rot, "QT")
                P.op("sp", lambda e, qi=qi, h=h, q0=q0, q1=q1: e.dma_start(
                    out=QT[qi][:], in_=g["QTm"].ap()[2 * h:2 * h + 2][:, :, q0:q1].rearrange("c p s -> p c s")),
                    (bAll[qg],), (b_QT[qi],), dma=True)
                pis = []
                for mb in range(2):
                    si = _rot(sps, rot, "sps")
                    for c in range(2):
                        P.op("pe", lambda e, si=si, c=c, mb=mb, h=h, qi=qi: e.matmul(
                            sps[si][:], KmT[:, 2 * h + c, mb * 128:(mb + 1) * 128], QT[qi][:, c, :],
                            start=(c == 0), stop=(c == 1)), (b_KmT, b_QT[qi]), (b_sps[si],))
                    pi = _rot(PT, rot, "PT")
                    pis.append(pi)
                    P.op("act", lambda e, si=si, pi=pi: e.activation(out=PT[pi][:], in_=sps[si][:], func=AF.Exp),
                         (b_sps[si],), (b_PT[pi],))
                di = _rot(dps, rot, "dps")
                for mb in range(2):
                    P.op("pe", lambda e, mb=mb, di=di, pi=pis[mb]: e.matmul(dps[di][:], ones, PT[pi][:],
                                                                           start=(mb == 0), stop=(mb == 1)),
                         (b_cb, b_PT[pis[mb]]), (b_dps[di],))
                for c2 in range(2):
                    gi = _rot(GG, rot, "GG")
                    P.op("sp", lambda e, gi=gi, h=h, c2=c2, q0=q0, q1=q1: e.dma_start(
                        out=GG[gi][:], in_=g["GT"].ap()[24 + 2 * h + c2][:, q0:q1]),
                        (bAll[qg],), (b_GG[gi],), dma=True)
                    oi = _rot(ops_, rot, "ops")
                    for mb in range(2):
                        P.op("pe", lambda e, mb=mb, oi=oi, h=h, c2=c2, pi=pis[mb]: e.matmul(
                            ops_[oi][:], Vm[:, mb, h * 256 + c2 * 128:h * 256 + (c2 + 1) * 128], PT[pi][:],
                            start=(mb == 0), stop=(mb == 1)), (b_Vm, b_PT[pis[mb]]), (b_ops[oi],))
                    fin(ops_[oi][:], b_ops[oi], (dps[di][:], b_dps[di]), gi, 24 + 2 * h + c2, qg, True)
        P.emit()


def stage_C(nc, P, l, S, env):
    g = env
    SFX = "_C%d" % l
    NG = S // 512
    x_cur, x_nxt = g["x_cur"], g["x_nxt"]
    with ExitStack() as st:
        sbt = lambda name, shape, dt: st.enter_context(nc.sbuf_tensor(name + SFX, list(shape), dt))
        pst = lambda name, shape, dt: st.enter_context(nc.psum_tensor(name + SFX, list(shape), dt))
        OG = sbt("OG", [128, 32, 512], BF16)
        MX = sbt("MX", [128, 32, 512], BF16)
        wst = [sbt("cwst%d" % i, [128, 4096], F32) for i in range(2)]
        wbf = [sbt("cwbf%d" % i, [128, 4096], BF16) for i in range(2)]
        RR = [sbt("RR%d" % i, [128, 3, 512], BF16) for i in range(2)]
        T = [sbt("T%d" % i, [128, 512], F32) for i in range(3)]
        xr = [sbt("xr%d" % i, [128, 256], F32) for i in range(3)]
        yo = [sbt("yo%d" % i, [128, 256], F32) for i in range(3)]
        ya = [pst("ya%d" % i, [128, 512], F32) for i in range(2)]
        yb = [pst("yb%d" % i, [128, 512], F32) for i in range(2)]
        ym = [pst("ym%d" % i, [128, 512], F32) for i in range(2)]
        op_ = [pst("op%d" % i, [128, 512], F32) for i in range(2)]
        B = lambda n: [Buf() for _ in range(n)]
        b_OG, b_MX = Buf(), Buf()
        b_wst, b_wbf, b_RR, b_T, b_xr, b_yo = B(2), B(2), B(2), B(3), B(3), B(3)
        b_ya, b_yb, b_ym, b_op = B(2), B(2), B(2), B(2)
        rot = {}
        P.barrier()

        def wload(src, n):
            i = _rot(wst, rot, "w")
            P.op("sp", lambda e: e.dma_start(out=wst[i][:, :n], in_=src), (), (b_wst[i],), dma=True)
            ce = "pool" if (rot["w"] % 2 == 0) else "dve"
            P.op(ce, lambda e: e.tensor_copy(out=wbf[i][:, :n], in_=wst[i][:, :n]), (b_wst[i],), (b_wbf[i],))
            return wbf[i], b_wbf[i]

        for tg in range(NG):
            t0_, t1_ = tg * 512, (tg + 1) * 512
            for c0 in range(0, 32, 8):
                P.op("sp", lambda e, c0=c0, t0_=t0_, t1_=t1_: e.dma_start(
                    out=OG[:, c0:c0 + 8, :], in_=g["OGT"].ap()[c0:c0 + 8, :, t0_:t1_].rearrange("c p s -> p c s")),
                    (g["b_OG"][tg],), (b_OG,), dma=True)
            for cc in range(32):
                pa = _rot(ya, rot, "ya")
                for (wt, nk, k0, ps, bps) in ((g["wPA"], 16, 0, ya, b_ya), (g["wPB"], 8, 16, yb, b_yb),
                                              (g["wPM"], 8, 24, ym, b_ym)):
                    w, bw = wload(wt.ap()[l, cc], nk * 128)
                    for kc in range(nk):
                        P.op("pe", lambda e, kc=kc, w=w, ps=ps, nk=nk, k0=k0, pa=pa: e.matmul(
                            ps[pa][:], w[:, kc * 128:(kc + 1) * 128], OG[:, k0 + kc, :], start=(kc == 0),
                            stop=(kc == nk - 1)), (bw, b_OG), (bps[pa],))
                ri = _rot(RR, rot, "RR")
                P.op("sp", lambda e, ri=ri, cc=cc, t0_=t0_, t1_=t1_: e.dma_start(
                    out=RR[ri][:], in_=g["RT"].ap()[cc:96:32][:, :, t0_:t1_].rearrange("c p s -> p c s")),
                    (g["b_A"][tg],), (b_RR[ri],), dma=True)
                P.op("dve", lambda e, ri=ri, pa=pa: e.tensor_tensor(out=T[0][:], in0=ya[pa][:], in1=RR[ri][:, 0, :],
                                                             op=ALU.mult), (b_ya[pa], b_RR[ri]), (b_T[0],))
                P.op("dve", lambda e, ri=ri, pa=pa: e.tensor_tensor(out=T[1][:], in0=yb[pa][:], in1=RR[ri][:, 1, :],
                                                             op=ALU.mult), (b_yb[pa], b_RR[ri]), (b_T[1],))
                P.op("dve", lambda e, ri=ri, pa=pa: e.tensor_tensor(out=T[2][:], in0=ym[pa][:], in1=RR[ri][:, 2, :],
                                                             op=ALU.mult), (b_ym[pa], b_RR[ri]), (b_T[2],))
                P.op("pool", lambda e: e.tensor_tensor(out=T[0][:], in0=T[0][:], in1=T[1][:], op=ALU.add),
                     (b_T[0], b_T[1]), (b_T[0],))
                P.op("pool", lambda e, cc=cc: e.tensor_tensor(out=MX[:, cc, :], in0=T[0][:], in1=T[2][:], op=ALU.add),
                     (b_T[0], b_T[2]), (b_MX,))
            for oc in range(16):
                ws = []
                for half in range(2):
                    ws.append(wload(g["wOUT"].ap()[l, oc][:, half * 4096:(half + 1) * 4096], 4096))
                    if half == 0:
                        pass
                for tb in range(4):
                    r0 = t0_ + tb * 128
                    xi = _rot(xr, rot, "xr")
                    P.op("sp", lambda e, xi=xi, r0=r0, oc=oc: e.dma_start(
                        out=xr[xi][:], in_=x_cur.ap()[r0:r0 + 128, oc * 256:(oc + 1) * 256]),
                        (g["bx"](x_cur, tg),), (b_xr[xi],), dma=True)
                    oi = _rot(op_, rot, "op")
                    for kc in range(32):
                        w, bw = ws[kc // 16]
                        k2 = kc % 16
                        P.op("pe", lambda e, kc=kc, k2=k2, w=w, oi=oi, tb=tb: e.matmul(
                            op_[oi][:, :256], MX[:, kc, tb * 128:(tb + 1) * 128], w[:, k2 * 256:(k2 + 1) * 256],
                            start=(kc == 0), stop=(kc == 31)), (bw, b_MX), (b_op[oi],))
                    yi = _rot(yo, rot, "yo")
                    P.op("dve", lambda e, oi=oi, xi=xi, yi=yi: e.tensor_tensor(out=yo[yi][:], in0=op_[oi][:, :256],
                                                                              in1=xr[xi][:], op=ALU.add),
                         (b_op[oi], b_xr[xi]), (b_yo[yi],))
                    P.op("sp", lambda e, yi=yi, r0=r0, oc=oc: e.dma_start(
                        out=x_nxt.ap()[r0:r0 + 128, oc * 256:(oc + 1) * 256], in_=yo[yi][:]),
                        (b_yo[yi],), (g["bx"](x_nxt, tg),), dma=True)
        P.emit()


def _tile(W, cw):
    K, N = W.shape
    return np.ascontiguousarray(W.reshape(K // 128, 128, N // cw, cw).transpose(2, 1, 0, 3)).reshape(
        N // cw, 128, (K // 128) * cw)


def _col(v):
    return np.ascontiguousarray(v.reshape(-1, 128).T)


_PERM = np.concatenate([np.arange(32, 64), np.arange(0, 32)])


def make_consts():
    c = np.zeros((128, NCONST), np.float32)
    p = np.arange(128)
    c[:, 0:128] = np.eye(128)
    c[:, 128:256] = 1.0
    c[:, 256:384] = -((p[:, None] >= p[None, :]).astype(np.float32))
    c[:, 384:512] = (p[:, None] % 64) == (p[None, :] % 64)
    k = p[:, None]
    i = p[None, :]
    dA = (k // 64) <= (i // 64)
    dB = k < i
    for j in range(4):
        for m in range(4):
            for (base, dm) in ((512, dA), (2560, dB)):
                blk = np.zeros((128, 128)) if m < j else (dm if m == j else np.ones((128, 128)))
                c[:, base + j * 512 + m * 128: base + j * 512 + (m + 1) * 128] = blk
    freq = (10000.0 ** (-(np.arange(32, dtype=np.float32)) / np.float32(32))).astype(np.float32)
    c[:, 4608] = freq[p % 32]
    ph = np.where(p < 64, 0.25, np.where(p < 96, 0.5, 0.0))
    c[:, 4609] = ph
    return c


def prep_weights(inp, DEPTH):
    out = {k: [] for k in ("gpack", "wF", "wVB", "wUQ", "wUK", "wUV", "wPA", "wPB", "wPM", "wOUT", "wMK", "wMV")}
    for l in range(DEPTH):
        w_in = np.asarray(inp["w_in"][l])
        kr = w_in[:, 1536:1600]
        colsF = np.concatenate([w_in[:, 0:1024], w_in[:, 1024:1536], kr, kr[:, _PERM], w_in[:, 1600:3648],
                                w_in[:, 6720:7744], w_in[:, 8768:9792], w_in[:, 3648:4672], w_in[:, 4672:5696],
                                w_in[:, 7744:8768], w_in[:, 9792:22080]], axis=1)
        out["wF"].append(_tile(colsF, 128))
        out["wVB"].append(_tile(w_in[:, 5696:6720], 256))
        w_uq = np.asarray(inp["w_uq"][l]).reshape(1024, 16, 192)
        nope = w_uq[:, :, 0:128].reshape(1024, 2048)
        rope = w_uq[:, :, 128:192]
        rope2 = np.concatenate([rope, rope[:, :, _PERM]], axis=2).reshape(1024, 2048)
        out["wUQ"].append(_tile(np.concatenate([nope, rope2], axis=1), 128))
        w_ukv = np.asarray(inp["w_ukv"][l]).reshape(512, 16, 256)
        out["wUK"].append(_tile(np.ascontiguousarray(w_ukv[:, :, 0:128]).reshape(512, 2048), 128))
        out["wUV"].append(_tile(np.ascontiguousarray(w_ukv[:, :, 128:256]).reshape(512, 2048), 256))
        out["wPA"].append(_tile(np.asarray(inp["w_pa"][l]), 128))
        out["wPB"].append(_tile(np.asarray(inp["w_pb"][l]), 128))
        out["wPM"].append(_tile(np.asarray(inp["w_pm"][l]), 128))
        out["wOUT"].append(_tile(np.asarray(inp["w_out"][l]), 256))
        w_mkv = np.asarray(inp["w_mkv"][l])
        out["wMK"].append(_tile(w_mkv[:, 0:1024], 128))
        out["wMV"].append(_tile(w_mkv[:, 1024:2048], 256))
        gq = np.asarray(inp["g_qn_a"][l])
        gk = np.asarray(inp["g_kn_a"][l])
        gpk = np.concatenate([
            _col(np.asarray(inp["g_pre"][l])), _col(np.asarray(inp["g_q_lat"][l])),
            _col(np.asarray(inp["g_kv_lat"][l])),
            gq[0:128, None], np.concatenate([gq[128:192], gq[128:192][_PERM]])[:, None],
            gk[0:128, None], np.concatenate([gk[128:192], gk[128:192][_PERM]])[:, None],
            _col(np.asarray(inp["g_mem"][l])), _col(np.asarray(inp["g_qn_m"][l])),
            _col(np.asarray(inp["g_kn_m"][l]))], axis=1).astype(np.float32)
        assert gpk.shape == (128, NG_COLS)
        out["gpack"].append(gpk)
    return {k: np.ascontiguousarray(np.stack(v)).astype(np.float32) for k, v in out.items()}


def run(inp, S, DEPTH, ncores, debug=()):
    nc = build(S, DEPTH, debug)
    wd = prep_weights(inp, DEPTH)
    consts = make_consts()
    in_maps = []
    for c in range(ncores):
        m = dict(wd)
        m["x"] = np.ascontiguousarray(np.asarray(inp["x"][c], dtype=np.float32))
        m["mem"] = np.ascontiguousarray(np.asarray(inp["mem"][c], dtype=np.float32))
        m["pos"] = np.ascontiguousarray(np.asarray(inp["positions"][c], dtype=np.int32).reshape(1, S))
        m["consts"] = consts
        in_maps.append(m)
    res = run_bass_kernel_spmd(nc, in_maps, core_ids=list(range(ncores)))
    return res


def kernel(**inputs):
    S, DEPTH = 8192, 4
    res = run(inputs, S, DEPTH, 2)
    return np.stack([res.results[c]["y"] for c in range(2)]).astype(np.float32)
```

```python
import math
from contextlib import ExitStack

import numpy as np
import concourse.bass as bass
import concourse.mybir as mybir
from concourse.bass_utils import run_bass_kernel_spmd

F32 = mybir.dt.float32
BF16 = mybir.dt.bfloat16
I32 = mybir.dt.int32
AF = mybir.ActivationFunctionType
ALU = mybir.AluOpType

D = 4096
EPS = 1e-6
NCH_F = 165
C_CQ, C_CKV, C_KR, C_GA, C_GB, C_GM, C_QB, C_KB, C_QM, C_MG = 0, 8, 12, 13, 29, 37, 45, 53, 61, 69
NCONST = 4610
NG_COLS = 84
ENGS = ("pe", "act", "dve", "pool", "sp")


class Buf:
    __slots__ = ("lw", "rd")

    def __init__(self):
        self.lw = None
        self.rd = []


class Op:
    __slots__ = ("eng", "fn", "deps", "dma", "signal", "sem", "key", "val", "emitted")

    def __init__(self, eng, fn, dma):
        self.eng = eng
        self.fn = fn
        self.dma = dma
        self.deps = []
        self.signal = False
        self.sem = None
        self.key = None
        self.val = None
        self.emitted = False


class Prog:
    def __init__(self, nc, es):
        self.nc = nc
        self.cur = {e: [] for e in ENGS}
        self.esem = {e: es.enter_context(nc.semaphore("s_" + e)) for e in ("pe", "act", "dve", "pool")}
        self.ecnt = {e: 0 for e in self.esem}
        self.K = {"sp": 12, "pool": 6, "act": 8}
        self.dsem = {q: [es.enter_context(nc.semaphore("d_%s%d" % (q, i))) for i in range(k)]
                     for q, k in self.K.items()}
        self.dcnt = {q: 0 for q in self.K}
        self.dlast = {q: [None] * k for q, k in self.K.items()}
        self.waited = {e: {} for e in ENGS}
        self.lastop = {e: None for e in ENGS}
        self.nops = 0

    def op(self, eng, fn, reads=(), writes=(), dma=False):
        o = Op(eng, fn, dma)
        self.nops += 1
        deps = []
        for b in reads:
            if b.lw is not None:
                deps.append((b.lw, True))
        for b in writes:
            if b.lw is not None:
                deps.append((b.lw, False))
            for r in b.rd:
                deps.append((r, False))
        seen = set()
        for d, raw in deps:
            if d is o or id(d) in seen:
                continue
            if (not d.dma) and (not dma) and d.eng == eng:
                if eng == "pe" or not raw:
                    continue
            seen.add(id(d))
            o.deps.append(d)
            if not d.emitted:
                d.signal = True
        if dma:
            k = self.dcnt[eng]
            kk = self.K[eng]
            slot = k % kk
            o.sem = self.dsem[eng][slot]
            o.key = "d_%s%d" % (eng, slot)
            o.val = 16 * (k // kk + 1)
            prev = self.dlast[eng][slot]
            if prev is not None:
                o.deps.append(prev)
            self.dlast[eng][slot] = o
            self.dcnt[eng] = k + 1
        else:
            o.sem = self.esem[eng]
            o.key = "s_" + eng
        for b in reads:
            b.rd.append(o)
        for b in writes:
            b.lw = o
            b.rd = []
        self.cur[eng].append(o)
        if fn is not None:
            self.lastop[eng] = o
        return o

    def barrier(self):
        tg = [o for o in self.lastop.values() if o is not None and not o.dma]
        for q in self.K:
            tg += [o for o in self.dlast[q] if o is not None]
        for o in tg:
            if not o.emitted:
                o.signal = True
        for e in ENGS:
            b = Op(e, None, False)
            b.deps = list(tg)
            self.cur[e].append(b)

    def emit(self):
        for e in ("pe", "act", "dve", "pool"):
            ops = self.cur[e]
            real = [o for o in ops if o.fn is not None and not o.dma]
            if real:
                real[-1].signal = True
            c = self.ecnt[e]
            for o in real:
                if o.signal:
                    c += 1
                    o.val = c
            nxt = None
            for o in reversed(real):
                if o.val is None:
                    o.val = nxt
                else:
                    nxt = o.val
            self.ecnt[e] = c
        with self.nc.Block() as block:
            decos = {"pe": block.tensor, "act": block.scalar, "dve": block.vector,
                     "pool": block.gpsimd, "sp": block.sync}
            for e in ENGS:
                ops = self.cur[e]
                if not ops:
                    continue

                def body(eng, ops=ops, e=e):
                    w = self.waited[e]
                    for o in ops:
                        for d in o.deps:
                            assert d.val is not None
                            if w.get(d.key, 0) < d.val:
                                eng.wait_ge(d.sem, d.val)
                                w[d.key] = d.val
                        if o.fn is not None:
                            ins = o.fn(eng)
                            if o.dma:
                                ins.then_inc(o.sem, 16)
                            elif o.signal:
                                ins.then_inc(o.sem, 1)
                        o.emitted = True

                decos[e](body)
        self.cur = {e: [] for e in ENGS}


def build(S, DEPTH, debug=(), stages="PABC"):
    nc = bass.Bass("TRN2", target_bir_lowering=False)
    NT = S // 128
    NG = S // 512
    TG = 512

    def din(name, shape, dt=F32):
        return nc.dram_tensor(name, list(shape), dt, kind="ExternalInput")

    def dscr(name, shape, dt):
        kind = "ExternalOutput" if name in debug else "Internal"
        return nc.dram_tensor(name, list(shape), dt, kind=kind)

    x_in = din("x", [S, D])
    mem_in = din("mem", [256, D])
    pos_in = din("pos", [1, S], I32)
    consts_in = din("consts", [128, NCONST])
    gpack_in = din("gpack", [DEPTH, 128, NG_COLS])
    wF = din("wF", [DEPTH, NCH_F, 128, 32 * 128])
    wVB = din("wVB", [DEPTH, 4, 128, 32 * 256])
    wUQ = din("wUQ", [DEPTH, 32, 128, 8 * 128])
    wUK = din("wUK", [DEPTH, 16, 128, 4 * 128])
    wUV = din("wUV", [DEPTH, 8, 128, 4 * 256])
    wPA = din("wPA", [DEPTH, 32, 128, 16 * 128])
    wPB = din("wPB", [DEPTH, 32, 128, 8 * 128])
    wPM = din("wPM", [DEPTH, 32, 128, 8 * 128])
    wOUT = din("wOUT", [DEPTH, 16, 128, 32 * 256])
    wMK = din("wMK", [DEPTH, 8, 128, 32 * 128])
    wMV = din("wMV", [DEPTH, 4, 128, 32 * 256])
    y_out = nc.dram_tensor("y", [S, D], F32, kind="ExternalOutput")

    QTa = dscr("QTa", [16, 2, 128, S], BF16)
    KTa = dscr("KTa", [16, 2, 128, S], BF16)
    Va = dscr("Va", [S, 2048], BF16)
    QTb = dscr("QTb", [8, 128, S], BF16)
    KTb = dscr("KTb", [8, 128, S], BF16)
    Vb = dscr("Vb", [S, 1024], BF16)
    QTm = dscr("QTm", [8, 128, S], BF16)
    GT = dscr("GT", [32, 128, S], BF16)
    RT = dscr("RT", [96, 128, S], BF16)
    OGT = dscr("OGT", [32, 128, S], BF16)
    CSd = dscr("CSd", [128, S], F32)
    xs = [dscr("xs0", [S, D], F32), dscr("xs1", [S, D], F32)]

    b_CS = Buf()
    b_x = {}

    def bx(t, g):
        k = (t.name, g)
        if k not in b_x:
            b_x[k] = Buf()
        return b_x[k]

    b_A = [Buf() for _ in range(NG)]
    b_OG = [Buf() for _ in range(NG)]

    with ExitStack() as es:
        P = Prog(nc, es)
        sb = lambda name, shape, dt: es.enter_context(nc.sbuf_tensor(name, list(shape), dt))
        cb = sb("cb", [128, 4608], BF16)
        fcol = sb("fcol", [128, 2], F32)
        gp = sb("gp", [128, NG_COLS], F32)
        KmT = sb("KmT", [128, 8, 256], BF16)
        Vm = sb("Vm", [128, 2, 1024], BF16)
        b_cb, b_fcol, b_gp, b_KmT, b_Vm = Buf(), Buf(), Buf(), Buf(), Buf()
        ident = cb[:, 0:128]
        ones = cb[:, 128:256]
        triN = cb[:, 256:384]
        dupI = cb[:, 384:512]
        maskA = [cb[:, 512 + j * 512: 512 + (j + 1) * 512] for j in range(4)]
        maskB = [cb[:, 2560 + j * 512: 2560 + (j + 1) * 512] for j in range(4)]

        with ExitStack() as st:
            tb_ = lambda name, shape, dt: st.enter_context(nc.sbuf_tensor(name, list(shape), dt))
            cst = tb_("cst", [128, NCONST], F32)
            posi = tb_("posi", [128, S], I32)
            t0 = tb_("t0", [128, S], F32)
            t1 = tb_("t1", [128, S], F32)
            ki = tb_("ki", [128, S], I32)
            b_cst, b_posi, b_t0, b_t1, b_ki = Buf(), Buf(), Buf(), Buf(), Buf()
            P.op("sp", lambda e: e.dma_start(out=cst[:], in_=consts_in.ap()), (), (b_cst,), dma=True)
            P.op("sp", lambda e: e.dma_start(out=posi[:], in_=bass.AP(pos_in, 0, [[0, 128], [1, S]])),
                 (), (b_posi,), dma=True)
            P.op("dve", lambda e: e.tensor_copy(out=cb[:], in_=cst[:, 0:4608]), (b_cst,), (b_cb,))
            P.op("dve", lambda e: e.tensor_copy(out=fcol[:], in_=cst[:, 4608:4610]), (b_cst,), (b_fcol,))
            P.op("dve", lambda e: e.tensor_copy(out=t0[:], in_=posi[:]), (b_posi,), (b_t0,))
            P.op("dve", lambda e: e.tensor_scalar(out=t0[:], in0=t0[:], scalar1=fcol[:, 0:1], scalar2=None,
                                                  op0=ALU.mult), (b_t0, b_fcol), (b_t0,))
            P.op("dve", lambda e: e.tensor_scalar(out=t0[:], in0=t0[:], scalar1=float(1.0 / (2 * math.pi)),
                                                  scalar2=None, op0=ALU.mult), (b_t0,), (b_t0,))
            P.op("dve", lambda e: e.tensor_scalar(out=t0[:], in0=t0[:], scalar1=fcol[:, 1:2], scalar2=None,
                                                  op0=ALU.add), (b_t0, b_fcol), (b_t0,))
            P.op("dve", lambda e: e.tensor_copy(out=ki[:], in_=t0[:]), (b_t0,), (b_ki,))
            P.op("dve", lambda e: e.tensor_copy(out=t1[:], in_=ki[:]), (b_ki,), (b_t1,))
            P.op("dve", lambda e: e.tensor_tensor(out=t0[:], in0=t0[:], in1=t1[:], op=ALU.subtract),
                 (b_t0, b_t1), (b_t0,))
            P.op("dve", lambda e: e.scalar_tensor_tensor(out=t1[:], in0=t0[:], scalar=0.5, in1=t0[:],
                                                         op0=ALU.is_gt, op1=ALU.subtract), (b_t0,), (b_t1,))
            P.op("dve", lambda e: e.tensor_scalar(out=t0[:], in0=t1[:], scalar1=0.5, scalar2=None, op0=ALU.is_gt),
                 (b_t1,), (b_t0,))
            P.op("dve", lambda e: e.tensor_tensor(out=t1[:], in0=t1[:], in1=t0[:], op=ALU.subtract),
                 (b_t0, b_t1), (b_t1,))
            P.op("act", lambda e: e.activation(out=t0[:], in_=t1[:], func=AF.Sin, scale=float(-2 * math.pi)),
                 (b_t1,), (b_t0,))
            P.op("sp", lambda e: e.dma_start(out=CSd.ap(), in_=t0[:]), (b_t0,), (b_CS,), dma=True)
            P.barrier()
            P.emit()

        wl = {"wF": wF, "wVB": wVB, "wUQ": wUQ, "wUK": wUK, "wUV": wUV, "wPA": wPA, "wPB": wPB, "wPM": wPM,
              "wOUT": wOUT, "wMK": wMK, "wMV": wMV}
        wb = {k: [nc.dram_tensor(k + "b%d" % l_, list(v.shape[1:]), BF16, kind="Internal") for l_ in range(DEPTH)]
              for k, v in wl.items()}
        with ExitStack() as st:
            c32 = [st.enter_context(nc.sbuf_tensor("c32_%d" % i, [128, 4096], F32)) for i in range(4)]
            c16 = [st.enter_context(nc.sbuf_tensor("c16_%d" % i, [128, 4096], BF16)) for i in range(4)]
            b32 = [Buf() for _ in range(4)]
            b16 = [Buf() for _ in range(4)]
            b_dummy = Buf()
            n_ = 0
            for l in range(DEPTH if "P" in stages else 0):
                for k, v in wl.items():
                    _, nch, _, n = v.shape
                    for ch in range(nch):
                        for p0 in range(0, n, 4096):
                            p1 = min(n, p0 + 4096)
                            i = n_ % 4
                            ce = ("dve", "act", "pool")[n_ % 3]
                            n_ += 1
                            src = v.ap()[l, ch][:, p0:p1]
                            dst = wb[k][l].ap()[ch][:, p0:p1]
                            P.op("sp", lambda e, i=i, src=src, p0=p0, p1=p1: e.dma_start(out=c32[i][:, :p1 - p0], in_=src),
                                 (), (b32[i],), dma=True)
                            if ce == "act":
                                P.op("act", lambda e, i=i, p0=p0, p1=p1: e.copy(out=c16[i][:, :p1 - p0], in_=c32[i][:, :p1 - p0]),
                                     (b32[i],), (b16[i],))
                            else:
                                P.op(ce, lambda e, i=i, p0=p0, p1=p1: e.tensor_copy(out=c16[i][:, :p1 - p0], in_=c32[i][:, :p1 - p0]),
                                     (b32[i],), (b16[i],))
                            P.op("act", lambda e, i=i, dst=dst, p0=p0, p1=p1: e.dma_start(out=dst, in_=c16[i][:, :p1 - p0]),
                                 (b16[i],), (b_dummy,), dma=True)
                            b_dummy.rd = []
                            b_dummy.lw = None
            P.barrier()
            P.emit()

        for l in range(DEPTH):
            x_cur = x_in if l == 0 else xs[(l - 1) % 2]
            x_nxt = y_out if l == DEPTH - 1 else xs[l % 2]
            if "A" in stages:
                stage_A(nc, P, l, S, locals())
            if "B" in stages:
                stage_B(nc, P, l, S, locals())
            if "C" in stages:
                stage_C(nc, P, l, S, locals())
        P.barrier()
        P.emit()
    return nc


def _rot(lst, state, key):
    i = state.get(key, 0)
    state[key] = i + 1
    return i % len(lst)


def stage_A(nc, P, l, S, env):
    g = env
    SFX = "_A%d" % l
    NG = S // 512
    cb, gp, fcol, KmT, Vm = g["cb"], g["gp"], g["fcol"], g["KmT"], g["Vm"]
    b_cb, b_gp, b_KmT, b_Vm = g["b_cb"], g["b_gp"], g["b_KmT"], g["b_Vm"]
    ident, ones, dupI = g["ident"], g["ones"], g["dupI"]
    x_cur = g["x_cur"]
    with ExitStack() as st:
        sbt = lambda name, shape, dt: st.enter_context(nc.sbuf_tensor(name + SFX, list(shape), dt))
        pst = lambda name, shape, dt: st.enter_context(nc.psum_tensor(name + SFX, list(shape), dt))
        hT = sbt("hT", [128, 32, 512], BF16)
        xt = sbt("xt", [128, D], F32)
        xn = sbt("xn", [128, D], BF16)
        wbf = [sbt("wbf%d" % i, [128, 4096], BF16) for i in range(4)]
        cq = sbt("cq", [128, 8, 512], BF16)
        ckv = sbt("ckv", [128, 4, 512], BF16)
        raw = [sbt("raw%d" % i, [128, 512], F32) for i in range(3)]
        sq = [sbt("sq%d" % i, [128, 512], BF16) for i in range(3)]
        so = [sbt("so%d" % i, [128, 512], BF16) for i in range(4)]
        rs = [sbt("rs%d" % i, [128, 512], F32) for i in range(2)]
        sskr = sbt("sskr", [128, 512], F32)
        kbase = sbt("kbase", [128, 512], F32)
        cst_ = sbt("cst_", [128, 512], F32)
        vst = [sbt("vst%d" % i, [128, 2048], BF16) for i in range(2)]
        sm = sbt("sm", [128, 8], F32)
        memT = hT
        acc = [pst("acc%d" % i, [128, 512], F32) for i in range(2)]
        ssb = [pst("ssb%d" % i, [128, 512], F32) for i in range(2)]
        trp = [pst("trp%d" % i, [128, 1024], BF16) for i in range(2)]
        tok = [pst("tok%d" % i, [128, 512], F32) for i in range(2)]
        B = lambda n: [Buf() for _ in range(n)]
        b_hT, b_xt, b_xn, b_cq, b_ckv, b_sskr, b_kbase, b_cst_, b_sm = (Buf() for _ in range(9))
        b_wbf, b_raw, b_sq, b_so, b_rs, b_vst = B(4), B(3), B(3), B(4), B(2), B(2)
        b_acc, b_ssb, b_trp, b_tok = B(2), B(2), B(2), B(2)
        rot = {}

        P.barrier()
        P.op("sp", lambda e: e.dma_start(out=gp[:], in_=g["gpack_in"].ap()[l]), (), (b_gp,), dma=True)

        def wload(src, n):
            i = _rot(wbf, rot, "w")
            P.op("sp", lambda e: e.dma_start(out=wbf[i][:, :n], in_=src), (), (b_wbf[i],), dma=True)
            return wbf[i], b_wbf[i]

        def norm_T(src_rows, nblk, dstT, b_dst, gcol0, src_bufs=()):
            for tb in range(nblk):
                P.op("sp", lambda e, tb=tb: e.dma_start(out=xt[:], in_=src_rows(tb)), tuple(src_bufs), (b_xt,), dma=True)
                P.op("act", lambda e: e.activation(out=xn[:], in_=xt[:], func=AF.Square, accum_out=sm[:, 0:1]),
                     (b_xt,), (b_xn, b_sm))
                P.op("act", lambda e: e.activation(out=sm[:, 1:2], in_=sm[:, 0:1], func=AF.Sqrt,
                                                   scale=1.0 / D, bias=EPS), (b_sm,), (b_sm,))
                P.op("dve", lambda e: e.reciprocal(out=sm[:, 2:3], in_=sm[:, 1:2]), (b_sm,), (b_sm,))
                P.op("act", lambda e: e.activation(out=xn[:], in_=xt[:], func=AF.Copy, scale=sm[:, 2:3]),
                     (b_xt, b_sm), (b_xn,))
                for q in range(4):
                    j = _rot(trp, rot, "trp")
                    for u in range(8):
                        kc = q * 8 + u
                        P.op("pe", lambda e, kc=kc, u=u, j=j: e.transpose(
                            out=trp[j][:, u * 128:(u + 1) * 128], in_=xn[:, kc * 128:(kc + 1) * 128],
                            identity=ident), (b_xn, b_cb), (b_trp[j],))
                    for u in range(8):
                        kc = q * 8 + u
                        P.op("dve", lambda e, kc=kc, u=u, j=j, tb=tb: e.tensor_scalar(
                            out=dstT[:, kc, tb * 128:(tb + 1) * 128], in0=trp[j][:, u * 128:(u + 1) * 128],
                            scalar1=gp[:, gcol0 + kc:gcol0 + kc + 1], scalar2=None, op0=ALU.mult),
                            (b_trp[j], b_gp), (b_dst,))

        def fm_mm(wsrc, nk, rhs_of, rhs_bufs, ntok, a):
            w, bw = wload(wsrc, nk * 128)
            for kc in range(nk):
                P.op("pe", lambda e, kc=kc: e.matmul(acc[a][:, :ntok], w[:, kc * 128:(kc + 1) * 128], rhs_of(kc),
                                                     start=(kc == 0), stop=(kc == nk - 1)),
                     (bw,) + tuple(rhs_bufs), (b_acc[a],))

        def rstd_from(ssp, b_ssp, n, scale, bias, ntok):
            P.op("act", lambda e: e.activation(out=rs[n][:, :ntok], in_=ssp, func=AF.Sqrt, scale=scale, bias=bias),
                 (b_ssp,), (b_rs[n],))
            P.op("dve", lambda e: e.reciprocal(out=rs[n][:, :ntok], in_=rs[n][:, :ntok]), (b_rs[n],), (b_rs[n],))

        def store(dst_ap, src_ap, bsrc, bdst):
            P.op("act", lambda e: e.dma_start(out=dst_ap, in_=src_ap), (bsrc,), (bdst,), dma=True)

        b_memT = b_hT
        norm_T(lambda tb: g["mem_in"].ap()[tb * 128:(tb + 1) * 128, :], 2, memT, b_memT, 48)
        rawm = sbt("rawm", [128, 8, 256], F32)
        b_rawm = Buf()
        for c in range(8):
            a = _rot(acc, rot, "acc")
            fm_mm(g["wb"]["wMK"][l].ap()[c], 32, lambda kc: memT[:, kc, 0:256], (b_memT,), 256, a)
            P.op("act", lambda e, c=c, a=a: e.activation(out=rawm[:, c, :], in_=acc[a][:, :256], func=AF.Copy),
                 (b_acc[a],), (b_rawm,))
            i = _rot(sq, rot, "sq")
            P.op("dve", lambda e, c=c, a=a, i=i: e.tensor_tensor(out=sq[i][:, :256], in0=acc[a][:, :256],
                                                                 in1=rawm[:, c, :], op=ALU.mult),
                 (b_acc[a], b_rawm), (b_sq[i],))
            h, cc = c // 2, c % 2
            P.op("pe", lambda e, i=i, cc=cc: e.matmul(ssb[0][:, :256], ones, sq[i][:, :256], start=(cc == 0),
                                                      stop=(cc == 1)), (b_sq[i], b_cb), (b_ssb[0],))
            if cc == 1:
                rstd_from(ssb[0][:, :256], b_ssb[0], 0, 1.0 / 256, EPS, 256)
                for c2 in (c - 1, c):
                    P.op("dve", lambda e, c2=c2: e.scalar_tensor_tensor(
                        out=KmT[:, c2, :], in0=rawm[:, c2, :], scalar=gp[:, 82 + c2 % 2:83 + c2 % 2],
                        in1=rs[0][:, :256], op0=ALU.mult, op1=ALU.mult), (b_rawm, b_gp, b_rs[0]), (b_KmT,))
        for mb in range(2):
            for vc in range(4):
                t = _rot(tok, rot, "tok")
                for half in range(2):
                    w, bw = wload(g["wb"]["wMV"][l].ap()[vc][:, half * 4096:(half + 1) * 4096], 4096)
                    for k2 in range(16):
                        kc = half * 16 + k2
                        P.op("pe", lambda e, kc=kc, k2=k2, t=t, mb=mb, w=w: e.matmul(
                            tok[t][:, :256], memT[:, kc, mb * 128:(mb + 1) * 128], w[:, k2 * 256:(k2 + 1) * 256],
                            start=(kc == 0), stop=(kc == 31)), (bw, b_memT), (b_tok[t],))
                P.op("act", lambda e, t=t, mb=mb, vc=vc: e.activation(
                    out=Vm[:, mb, vc * 256:(vc + 1) * 256], in_=tok[t][:, :256], func=AF.Copy),
                    (b_tok[t],), (b_Vm,))

        for tg in range(NG):
            t0_, t1_ = tg * 512, (tg + 1) * 512
            bA = g["b_A"][tg]
            norm_T(lambda tb, t0_=t0_: x_cur.ap()[t0_ + tb * 128: t0_ + (tb + 1) * 128, :], 4, hT, b_hT, 0, (g["bx"](x_cur, tg),))
            P.op("sp", lambda e, t0_=t0_, t1_=t1_: e.dma_start(out=cst_[:], in_=g["CSd"].ap()[:, t0_:t1_]), (g["b_CS"],), (b_cst_,),
                 dma=True)
            hrhs = lambda kc: hT[:, kc, :]

            for (c0, n, dst, b_dst, gc, ssi) in ((C_CQ, 8, cq, b_cq, 32, 0), (C_CKV, 4, ckv, b_ckv, 40, 1)):
                for c in range(n):
                    a = _rot(acc, rot, "acc")
                    fm_mm(g["wb"]["wF"][l].ap()[c0 + c], 32, hrhs, (b_hT,), 512, a)
                    r = _rot(raw, rot, "raw")
                    P.op("act", lambda e, a=a, r=r: e.activation(out=raw[r][:], in_=acc[a][:], func=AF.Copy),
                         (b_acc[a],), (b_raw[r],))
                    i = _rot(sq, rot, "sq")
                    P.op("dve", lambda e, a=a, r=r, i=i: e.tensor_tensor(out=sq[i][:], in0=acc[a][:], in1=raw[r][:],
                                                                         op=ALU.mult),
                         (b_acc[a], b_raw[r]), (b_sq[i],))
                    P.op("pool", lambda e, r=r, c=c, dst=dst: e.tensor_copy(out=dst[:, c, :], in_=raw[r][:]),
                         (b_raw[r],), (b_dst,))
                    P.op("pe", lambda e, i=i, c=c, n=n, ssi=ssi: e.matmul(ssb[ssi][:], ones, sq[i][:],
                                                                         start=(c == 0), stop=(c == n - 1)),
                         (b_sq[i], b_cb), (b_ssb[ssi],))
                rstd_from(ssb[ssi][:], b_ssb[ssi], ssi, 1.0 / (n * 128), EPS, 512)
                for c in range(n):
                    P.op("dve", lambda e, c=c, dst=dst, gc=gc, ssi=ssi: e.scalar_tensor_tensor(
                        out=dst[:, c, :], in0=dst[:, c, :], scalar=gp[:, gc + c:gc + c + 1], in1=rs[ssi][:],
                        op0=ALU.mult, op1=ALU.mult), (b_dst, b_gp, b_rs[ssi]), (b_dst,))

            a = _rot(acc, rot, "acc")
            fm_mm(g["wb"]["wF"][l].ap()[C_KR], 32, hrhs, (b_hT,), 512, a)
            r = _rot(raw, rot, "raw")
            P.op("act", lambda e, a=a, r=r: e.activation(out=raw[r][:], in_=acc[a][:], func=AF.Copy),
                 (b_acc[a],), (b_raw[r],))
            i = _rot(sq, rot, "sq")
            P.op("dve", lambda e, a=a, r=r, i=i: e.tensor_tensor(out=sq[i][:], in0=acc[a][:], in1=raw[r][:],
                                                                 op=ALU.mult), (b_acc[a], b_raw[r]), (b_sq[i],))
            P.op("pe", lambda e, i=i: e.matmul(ssb[0][:], ones[0:64, :], sq[i][0:64, :], start=True, stop=True),
                 (b_sq[i], b_cb), (b_ssb[0],))
            P.op("act", lambda e: e.activation(out=sskr[:], in_=ssb[0][:], func=AF.Copy), (b_ssb[0],), (b_sskr,))
            i2 = _rot(sq, rot, "sq")
            P.op("dve", lambda e, r=r, i2=i2: e.scalar_tensor_tensor(
                out=sq[i2][:], in0=raw[r][:], scalar=gp[:, 47:48], in1=cst_[:], op0=ALU.mult, op1=ALU.mult),
                (b_raw[r], b_gp, b_cst_), (b_sq[i2],))
            P.op("pe", lambda e, i2=i2: e.matmul(ssb[1][:], dupI, sq[i2][:], start=True, stop=True),
                 (b_sq[i2], b_cb), (b_ssb[1],))
            P.op("act", lambda e: e.activation(out=kbase[:], in_=ssb[1][:], func=AF.Copy), (b_ssb[1],), (b_kbase,))

            for h in range(16):
                an = _rot(acc, rot, "acc")
                fm_mm(g["wb"]["wUQ"][l].ap()[h], 8, lambda kc: cq[:, kc, :], (b_cq,), 512, an)
                rn = _rot(raw, rot, "raw")
                P.op("act", lambda e, an=an, rn=rn: e.activation(out=raw[rn][:], in_=acc[an][:], func=AF.Copy),
                     (b_acc[an],), (b_raw[rn],))
                i = _rot(sq, rot, "sq")
                P.op("dve", lambda e, an=an, rn=rn, i=i: e.tensor_tensor(out=sq[i][:], in0=acc[an][:],
                                                                         in1=raw[rn][:], op=ALU.mult),
                     (b_acc[an], b_raw[rn]), (b_sq[i],))
                P.op("pe", lambda e, i=i: e.matmul(ssb[0][:], ones, sq[i][:], start=True, stop=False),
                     (b_sq[i], b_cb), (b_ssb[0],))
                ar = _rot(acc, rot, "acc")
                fm_mm(g["wb"]["wUQ"][l].ap()[16 + h], 8, lambda kc: cq[:, kc, :], (b_cq,), 512, ar)
                rr = _rot(raw, rot, "raw")
                P.op("act", lambda e, ar=ar, rr=rr: e.activation(out=raw[rr][:], in_=acc[ar][:], func=AF.Copy),
                     (b_acc[ar],), (b_raw[rr],))
                i = _rot(sq, rot, "sq")
                P.op("dve", lambda e, ar=ar, rr=rr, i=i: e.tensor_tensor(out=sq[i][:], in0=acc[ar][:],
                                                                         in1=raw[rr][:], op=ALU.mult),
                     (b_acc[ar], b_raw[rr]), (b_sq[i],))
                P.op("pe", lambda e, i=i: e.matmul(ssb[0][:], ones[0:64, :], sq[i][0:64, :], start=False, stop=True),
                     (b_sq[i], b_cb), (b_ssb[0],))
                rstd_from(ssb[0][:], b_ssb[0], 0, 1.0, 192 * EPS, 512)
                o1 = _rot(so, rot, "so")
                P.op("dve", lambda e, rn=rn, o1=o1: e.scalar_tensor_tensor(
                    out=so[o1][:], in0=raw[rn][:], scalar=gp[:, 44:45], in1=rs[0][:], op0=ALU.mult, op1=ALU.mult),
                    (b_raw[rn], b_gp, b_rs[0]), (b_so[o1],))
                store(g["QTa"].ap()[h, 0][:, t0_:t1_], so[o1][:], b_so[o1], bA)
                P.op("dve", lambda e, rr=rr: e.scalar_tensor_tensor(
                    out=raw[rr][:], in0=raw[rr][:], scalar=gp[:, 45:46], in1=rs[0][:], op0=ALU.mult, op1=ALU.mult),
                    (b_raw[rr], b_gp, b_rs[0]), (b_raw[rr],))
                o2 = _rot(so, rot, "so")
                P.op("pool", lambda e, rr=rr, o2=o2: e.tensor_tensor(out=so[o2][:], in0=raw[rr][:], in1=cst_[:],
                                                                     op=ALU.mult),
                     (b_raw[rr], b_cst_), (b_so[o2],))
                store(g["QTa"].ap()[h, 1][:, t0_:t1_], so[o2][:], b_so[o2], bA)

            for h in range(16):
                a = _rot(acc, rot, "acc")
                fm_mm(g["wb"]["wUK"][l].ap()[h], 4, lambda kc: ckv[:, kc, :], (b_ckv,), 512, a)
                r = _rot(raw, rot, "raw")
                P.op("act", lambda e, a=a, r=r: e.activation(out=raw[r][:], in_=acc[a][:], func=AF.Copy),
                     (b_acc[a],), (b_raw[r],))
                i = _rot(sq, rot, "sq")
                P.op("dve", lambda e, a=a, r=r, i=i: e.tensor_tensor(out=sq[i][:], in0=acc[a][:], in1=raw[r][:],
                                                                     op=ALU.mult), (b_acc[a], b_raw[r]), (b_sq[i],))
                P.op("pe", lambda e, i=i: e.matmul(ssb[1][:], ones, sq[i][:], start=True, stop=True),
                     (b_sq[i], b_cb), (b_ssb[1],))
                P.op("dve", lambda e: e.tensor_tensor(out=rs[1][:], in0=ssb[1][:], in1=sskr[:], op=ALU.add),
                     (b_ssb[1], b_sskr), (b_rs[1],))
                rstd_from(rs[1][:], b_rs[1], 1, 1.0 / 192, EPS, 512)
                o1 = _rot(so, rot, "so")
                P.op("dve", lambda e, r=r, o1=o1: e.scalar_tensor_tensor(
                    out=so[o1][:], in0=raw[r][:], scalar=gp[:, 46:47], in1=rs[1][:], op0=ALU.mult, op1=ALU.mult),
                    (b_raw[r], b_gp, b_rs[1]), (b_so[o1],))
                store(g["KTa"].ap()[h, 0][:, t0_:t1_], so[o1][:], b_so[o1], bA)
                o2 = _rot(so, rot, "so")
                P.op("pool", lambda e, o2=o2: e.tensor_tensor(out=so[o2][:], in0=kbase[:], in1=rs[1][:], op=ALU.mult),
                     (b_kbase, b_rs[1]), (b_so[o2],))
                store(g["KTa"].ap()[h, 1][:, t0_:t1_], so[o2][:], b_so[o2], bA)

            for tb in range(4):
                v = _rot(vst, rot, "vst")
                for vc in range(8):
                    t = _rot(tok, rot, "tok")
                    w, bw = wload(g["wb"]["wUV"][l].ap()[vc], 1024)
                    for kc in range(4):
                        P.op("pe", lambda e, kc=kc, t=t, tb=tb, w=w: e.matmul(
                            tok[t][:, :256], ckv[:, kc, tb * 128:(tb + 1) * 128], w[:, kc * 256:(kc + 1) * 256],
                            start=(kc == 0), stop=(kc == 3)), (bw, b_ckv), (b_tok[t],))
                    P.op("act", lambda e, t=t, v=v, vc=vc: e.activation(
                        out=vst[v][:, vc * 256:(vc + 1) * 256], in_=tok[t][:, :256], func=AF.Copy),
                        (b_tok[t],), (b_vst[v],))
                store(g["Va"].ap()[t0_ + tb * 128:t0_ + (tb + 1) * 128, :], vst[v][:], b_vst[v], bA)

            for tb in range(4):
                v = _rot(vst, rot, "vst")
                for vc in range(4):
                    t = _rot(tok, rot, "tok")
                    for half in range(2):
                        w, bw = wload(g["wb"]["wVB"][l].ap()[vc][:, half * 4096:(half + 1) * 4096], 4096)
                        for k2 in range(16):
                            kc = half * 16 + k2
                            P.op("pe", lambda e, kc=kc, k2=k2, t=t, tb=tb, w=w: e.matmul(
                                tok[t][:, :256], hT[:, kc, tb * 128:(tb + 1) * 128], w[:, k2 * 256:(k2 + 1) * 256],
                                start=(kc == 0), stop=(kc == 31)), (bw, b_hT), (b_tok[t],))
                    P.op("act", lambda e, t=t, v=v, vc=vc: e.activation(
                        out=vst[v][:, vc * 256:(vc + 1) * 256], in_=tok[t][:, :256], func=AF.Copy),
                        (b_tok[t],), (b_vst[v],))
                store(g["Vb"].ap()[t0_ + tb * 128:t0_ + (tb + 1) * 128, :], vst[v][:, 0:1024], b_vst[v], bA)

            simple = []
            for c in range(32):
                simple.append((C_GA + c, AF.Silu, 1.0, GTd(g)[c]))
            for c in range(8):
                simple.append((C_QB + c, AF.Copy, 1.0 / math.sqrt(128.0), g["QTb"].ap()[c]))
            for c in range(8):
                simple.append((C_KB + c, AF.Copy, 1.0, g["KTb"].ap()[c]))
            for c in range(96):
                simple.append((C_MG + c, AF.Sigmoid, 1.0, g["RT"].ap()[c]))
            for (ci, fn_, sc, dst) in simple:
                a = _rot(acc, rot, "acc")
                fm_mm(g["wb"]["wF"][l].ap()[ci], 32, hrhs, (b_hT,), 512, a)
                o1 = _rot(so, rot, "so")
                if fn_ == AF.Copy:
                    P.op("act", lambda e, a=a, o1=o1, sc=sc: e.mul(out=so[o1][:], in_=acc[a][:], mul=sc),
                         (b_acc[a],), (b_so[o1],))
                else:
                    P.op("act", lambda e, a=a, o1=o1, fn_=fn_: e.activation(out=so[o1][:], in_=acc[a][:], func=fn_),
                         (b_acc[a],), (b_so[o1],))
                store(dst[:, t0_:t1_], so[o1][:], b_so[o1], bA)

            for h in range(4):
                rr_ = []
                for cc in range(2):
                    a = _rot(acc, rot, "acc")
                    fm_mm(g["wb"]["wF"][l].ap()[C_QM + 2 * h + cc], 32, hrhs, (b_hT,), 512, a)
                    r = _rot(raw, rot, "raw")
                    rr_.append(r)
                    P.op("act", lambda e, a=a, r=r: e.activation(out=raw[r][:], in_=acc[a][:], func=AF.Copy),
                         (b_acc[a],), (b_raw[r],))
                    i = _rot(sq, rot, "sq")
                    P.op("dve", lambda e, a=a, r=r, i=i: e.tensor_tensor(out=sq[i][:], in0=acc[a][:], in1=raw[r][:],
                                                                         op=ALU.mult),
                         (b_acc[a], b_raw[r]), (b_sq[i],))
                    P.op("pe", lambda e, i=i, cc=cc: e.matmul(ssb[0][:], ones, sq[i][:], start=(cc == 0),
                                                              stop=(cc == 1)), (b_sq[i], b_cb), (b_ssb[0],))
                rstd_from(ssb[0][:], b_ssb[0], 0, 1.0, 256 * EPS, 512)
                for cc in range(2):
                    o1 = _rot(so, rot, "so")
                    r = rr_[cc]
                    P.op("dve", lambda e, r=r, o1=o1, cc=cc: e.scalar_tensor_tensor(
                        out=so[o1][:], in0=raw[r][:], scalar=gp[:, 80 + cc:81 + cc], in1=rs[0][:],
                        op0=ALU.mult, op1=ALU.mult), (b_raw[r], b_gp, b_rs[0]), (b_so[o1],))
                    store(g["QTm"].ap()[2 * h + cc][:, t0_:t1_], so[o1][:], b_so[o1], bA)
        P.emit()


def GTd(g):
    return [g["GT"].ap()[c] for c in range(32)]


def stage_B(nc, P, l, S, env):
    g = env
    SFX = "_B%d" % l
    NT = S // 128
    NG = S // 512
    cb, KmT, Vm = g["cb"], g["KmT"], g["Vm"]
    b_cb, b_KmT, b_Vm = g["b_cb"], g["b_KmT"], g["b_Vm"]
    ones, triN = g["ones"], g["triN"]
    maskA, maskB = g["maskA"], g["maskB"]
    with ExitStack() as st:
        sbt = lambda name, shape, dt: st.enter_context(nc.sbuf_tensor(name + SFX, list(shape), dt))
        pst = lambda name, shape, dt: st.enter_context(nc.psum_tensor(name + SFX, list(shape), dt))
        KT = [sbt("KT%d" % i, [128, 2, S], BF16) for i in range(2)]
        VV = [sbt("VV%d" % i, [128, NT, 128], BF16) for i in range(2)]
        QT = [sbt("QT%d" % i, [128, 2, 512], BF16) for i in range(2)]
        GG = [sbt("GG%d" % i, [128, 512], BF16) for i in range(2)]
        PT = [sbt("PT%d" % i, [128, 512], BF16) for i in range(3)]
        LT = [sbt("LT%d" % i, [128, 512], BF16) for i in range(2)]
        EE = [sbt("EE%d" % i, [128, 512], F32) for i in range(2)]
        AR = [sbt("AR%d" % i, [128, 512], F32) for i in range(2)]
        Rsb = sbt("Rsb", [128, 512], F32)
        rden = sbt("rden", [128, 512], F32)
        of = sbt("of", [128, 512], F32)
        ob = [sbt("ob%d" % i, [128, 512], BF16) for i in range(2)]
        sps = [pst("sps%d" % i, [128, 512], F32) for i in range(3)]
        ops_ = [pst("ops%d" % i, [128, 512], F32) for i in range(2)]
        dps = [pst("dps%d" % i, [128, 512], F32) for i in range(2)]
        tps = pst("tps", [128, 512], F32)
        B = lambda n: [Buf() for _ in range(n)]
        b_KT, b_VV, b_QT, b_GG, b_PT, b_LT, b_EE, b_AR, b_ob = B(2), B(2), B(2), B(2), B(3), B(2), B(2), B(2), B(2)
        b_sps, b_ops, b_dps = B(3), B(2), B(2)
        b_Rsb, b_rden, b_of, b_tps = Buf(), Buf(), Buf(), Buf()
        rot = {}
        bAll = g["b_A"]
        P.barrier()

        def fin(osrc, b_osrc, den, gi, chunk, qg, norm):
            q0, q1 = qg * 512, (qg + 1) * 512
            if norm:
                P.op("dve", lambda e: e.reciprocal(out=rden[:], in_=den[0]), (den[1],), (b_rden,))
                P.op("dve", lambda e: e.tensor_tensor(out=of[:], in0=osrc, in1=rden[:], op=ALU.mult),
                     (b_osrc, b_rden), (b_of,))
                o = _rot(ob, rot, "ob")
                P.op("dve", lambda e, o=o: e.tensor_tensor(out=ob[o][:], in0=of[:], in1=GG[gi][:], op=ALU.mult),
                     (b_of, b_GG[gi]), (b_ob[o],))
            else:
                o = _rot(ob, rot, "ob")
                P.op("dve", lambda e, o=o: e.tensor_tensor(out=ob[o][:], in0=osrc, in1=GG[gi][:], op=ALU.mult),
                     (b_osrc, b_GG[gi]), (b_ob[o],))
            P.op("act", lambda e, o=o: e.dma_start(out=g["OGT"].ap()[chunk][:, q0:q1], in_=ob[o][:]),
                 (b_ob[o],), (g["b_OG"][qg],), dma=True)

        for h in range(16):
            kb_ = _rot(KT, rot, "KT")
            P.op("sp", lambda e, kb_=kb_, h=h: e.dma_start(out=KT[kb_][:], in_=g["KTa"].ap()[h].rearrange("c p s -> p c s")),
                 tuple(bAll), (b_KT[kb_],), dma=True)
            for b0 in range(0, NT, 8):
                b1 = min(NT, b0 + 8)
                P.op("sp", lambda e, kb_=kb_, h=h, b0=b0, b1=b1: e.dma_start(
                    out=VV[kb_][:, b0:b1, :],
                    in_=g["Va"].ap()[b0 * 128:b1 * 128, h * 128:(h + 1) * 128].rearrange("(b p) d -> p b d", p=128)),
                    tuple(bAll), (b_VV[kb_],), dma=True)
            for qg in range(NG):
                q0, q1 = qg * 512, (qg + 1) * 512
                qi = _rot(QT, rot, "QT")
                P.op("sp", lambda e, qi=qi, h=h, q0=q0, q1=q1: e.dma_start(
                    out=QT[qi][:], in_=g["QTa"].ap()[h][:, :, q0:q1].rearrange("c p s -> p c s")),
                    (bAll[qg],), (b_QT[qi],), dma=True)
                gi = _rot(GG, rot, "GG")
                P.op("sp", lambda e, gi=gi, h=h, q0=q0, q1=q1: e.dma_start(out=GG[gi][:], in_=g["GT"].ap()[h][:, q0:q1]),
                     (bAll[qg],), (b_GG[gi],), dma=True)
                oi = _rot(ops_, rot, "ops")
                nkb = 4 * qg + 4
                def qk_a(kb, kb_=kb_, qi=qi):
                    si = _rot(sps, rot, "sps")
                    for c in range(2):
                        P.op("pe", lambda e, si=si, c=c, kb=kb, kb_=kb_, qi=qi: e.matmul(
                            sps[si][:], KT[kb_][:, c, kb * 128:(kb + 1) * 128], QT[qi][:, c, :],
                            start=(c == 0), stop=(c == 1)), (b_KT[kb_], b_QT[qi]), (b_sps[si],))
                    return si

                si_next = qk_a(0)
                for kb in range(nkb):
                    si = si_next
                    if kb + 1 < nkb:
                        si_next = qk_a(kb + 1)
                    pi = _rot(PT, rot, "PT")
                    P.op("act", lambda e, si=si, pi=pi: e.activation(out=PT[pi][:], in_=sps[si][:], func=AF.Exp),
                         (b_sps[si],), (b_PT[pi],))
                    j = kb - 4 * qg
                    if j >= 0:
                        P.op("dve", lambda e, pi=pi, j=j: e.tensor_tensor(out=PT[pi][:], in0=PT[pi][:], in1=maskA[j],
                                                                           op=ALU.mult),
                             (b_PT[pi], b_cb), (b_PT[pi],))
                    P.op("pe", lambda e, pi=pi, kb=kb, kb_=kb_, oi=oi, nkb=nkb: e.matmul(
                        ops_[oi][:], VV[kb_][:, kb, :], PT[pi][:], start=(kb == 0), stop=(kb == nkb - 1)),
                        (b_VV[kb_], b_PT[pi]), (b_ops[oi],))
                    P.op("pe", lambda e, pi=pi, kb=kb, oi=oi, nkb=nkb: e.matmul(
                        dps[oi][:], ones, PT[pi][:], start=(kb == 0), stop=(kb == nkb - 1)),
                        (b_cb, b_PT[pi]), (b_dps[oi],))
                fin(ops_[oi][:], b_ops[oi], (dps[oi][:], b_dps[oi]), gi, h, qg, True)

        for h in range(8):
            kb_ = _rot(KT, rot, "KT")
            P.op("sp", lambda e, kb_=kb_, h=h: e.dma_start(out=KT[kb_][:, 0, :], in_=g["KTb"].ap()[h]),
                 tuple(bAll), (b_KT[kb_],), dma=True)
            for b0 in range(0, NT, 8):
                b1 = min(NT, b0 + 8)
                P.op("sp", lambda e, kb_=kb_, h=h, b0=b0, b1=b1: e.dma_start(
                    out=VV[kb_][:, b0:b1, :],
                    in_=g["Vb"].ap()[b0 * 128:b1 * 128, h * 128:(h + 1) * 128].rearrange("(b p) d -> p b d", p=128)),
                    tuple(bAll), (b_VV[kb_],), dma=True)
            for qg in range(NG):
                q0, q1 = qg * 512, (qg + 1) * 512
                qi = _rot(QT, rot, "QT")
                P.op("sp", lambda e, qi=qi, h=h, q0=q0, q1=q1: e.dma_start(out=QT[qi][:, 0, :],
                                                                          in_=g["QTb"].ap()[h][:, q0:q1]),
                     (bAll[qg],), (b_QT[qi],), dma=True)
                gi = _rot(GG, rot, "GG")
                P.op("sp", lambda e, gi=gi, h=h, q0=q0, q1=q1: e.dma_start(out=GG[gi][:],
                                                                          in_=g["GT"].ap()[16 + h][:, q0:q1]),
                     (bAll[qg],), (b_GG[gi],), dma=True)
                oi = _rot(ops_, rot, "ops")
                nkb = 4 * qg + 4
                def p1(n_, kb_=kb_, qi=qi, nkb=nkb, qg=qg):
                    kb = nkb - 1 - n_
                    j = kb - 4 * qg
                    si = _rot(sps, rot, "sps")
                    P.op("pe", lambda e, si=si, kb=kb, kb_=kb_, qi=qi: e.matmul(
                        sps[si][:], KT[kb_][:, 0, kb * 128:(kb + 1) * 128], QT[qi][:, 0, :], start=True, stop=False),
                        (b_KT[kb_], b_QT[qi]), (b_sps[si],))
                    ei = _rot(EE, rot, "EE")
                    P.op("act", lambda e, si=si, ei=ei: e.activation(out=EE[ei][:], in_=sps[si][:], func=AF.Exp),
                         (b_sps[si],), (b_EE[ei],))
                    li = _rot(LT, rot, "LT")
                    P.op("act", lambda e, ei=ei, li=li: e.activation(out=LT[li][:], in_=EE[ei][:], func=AF.Ln, bias=1.0),
                         (b_EE[ei],), (b_LT[li],))
                    if j >= 0:
                        P.op("dve", lambda e, li=li, j=j: e.tensor_tensor(out=LT[li][:], in0=LT[li][:], in1=maskB[j],
                                                                           op=ALU.mult),
                             (b_LT[li], b_cb), (b_LT[li],))
                    P.op("pe", lambda e, si=si, li=li: e.matmul(sps[si][:], triN, LT[li][:], start=False, stop=True,
                                                                skip_group_check=True),
                         (b_cb, b_LT[li]), (b_sps[si],))
                    ti = n_ % 2
                    if n_ < nkb - 1:
                        P.op("pe", lambda e, li=li, ti=ti: e.matmul(dps[ti][:], ones, LT[li][:], start=True, stop=True),
                             (b_cb, b_LT[li]), (b_dps[ti],))
                    return si, ti

                st_next = p1(0)
                for n_ in range(nkb):
                    kb = nkb - 1 - n_
                    j = kb - 4 * qg
                    si, ti = st_next
                    if n_ + 1 < nkb:
                        st_next = p1(n_ + 1)
                    ai = _rot(AR, rot, "AR")
                    if n_ == 0:
                        P.op("dve", lambda e, si=si, ai=ai: e.tensor_copy(out=AR[ai][:], in_=sps[si][:]),
                             (b_sps[si],), (b_AR[ai],))
                    else:
                        P.op("dve", lambda e, si=si, ai=ai: e.tensor_tensor(out=AR[ai][:], in0=sps[si][:], in1=Rsb[:],
                                                                            op=ALU.subtract),
                             (b_sps[si], b_Rsb), (b_AR[ai],))
                    pi = _rot(PT, rot, "PT")
                    P.op("act", lambda e, ai=ai, pi=pi: e.activation(out=PT[pi][:], in_=AR[ai][:], func=AF.Exp),
                         (b_AR[ai],), (b_PT[pi],))
                    if j >= 0:
                        P.op("dve", lambda e, pi=pi, j=j: e.tensor_tensor(out=PT[pi][:], in0=PT[pi][:], in1=maskB[j],
                                                                           op=ALU.mult),
                             (b_PT[pi], b_cb), (b_PT[pi],))
                    P.op("pe", lambda e, pi=pi, kb=kb, kb_=kb_, oi=oi, n_=n_, nkb=nkb: e.matmul(
                        ops_[oi][:], VV[kb_][:, kb, :], PT[pi][:], start=(n_ == 0), stop=(n_ == nkb - 1)),
                        (b_VV[kb_], b_PT[pi]), (b_ops[oi],))
                    if n_ < nkb - 1:
                        if n_ == 0:
                            P.op("dve", lambda e, ti=ti: e.tensor_copy(out=Rsb[:], in_=dps[ti][:]), (b_dps[ti],), (b_Rsb,))
                        else:
                            P.op("dve", lambda e, ti=ti: e.tensor_tensor(out=Rsb[:], in0=dps[ti][:], in1=Rsb[:], op=ALU.add),
                                 (b_dps[ti], b_Rsb), (b_Rsb,))
                fin(ops_[oi][:], b_ops[oi], None, gi, 16 + h, qg, False)

        for qg in range(NG):
            q0, q1 = qg * 512, (qg + 1) * 512
            for h in range(4):
                qi = _rot(QT, rot, "QT")
                P.op("sp", lambda e, qi=qi, h=h, q0=q0, q1=q1: e.dma_start(
                    out=QT[qi][:], in_=g["QTm"].ap()[2 * h:2 * h + 2][:, :, q0:q1].rearrange("c p s -> p c s")),
                    (bAll[qg],), (b_QT[qi],), dma=True)
                pis = []
                for mb in range(2):
                    si = _rot(sps, rot, "sps")
                    for c in range(2):
                        P.op("pe", lambda e, si=si, c=c, mb=mb, h=h, qi=qi: e.matmul(
                            sps[si][:], KmT[:, 2 * h + c, mb * 128:(mb + 1) * 128], QT[qi][:, c, :],
                            start=(c == 0), stop=(c == 1)), (b_KmT, b_QT[qi]), (b_sps[si],))
                    pi = _rot(PT, rot, "PT")
                    pis.append(pi)
                    P.op("act", lambda e, si=si, pi=pi: e.activation(out=PT[pi][:], in_=sps[si][:], func=AF.Exp),
                         (b_sps[si],), (b_PT[pi],))
                di = _rot(dps, rot, "dps")
                for mb in range(2):
                    P.op("pe", lambda e, mb=mb, di=di, pi=pis[mb]: e.matmul(dps[di][:], ones, PT[pi][:],
                                                                           start=(mb == 0), stop=(mb == 1)),
                         (b_cb, b_PT[pis[mb]]), (b_dps[di],))
                for c2 in range(2):
                    gi = _rot(GG, rot, "GG")
                    P.op("sp", lambda e, gi=gi, h=h, c2=c2, q0=q0, q1=q1: e.dma_start(
                        out=GG[gi][:], in_=g["GT"].ap()[24 + 2 * h + c2][:, q0:q1]),
                        (bAll[qg],), (b_GG[gi],), dma=True)
                    oi = _rot(ops_, rot, "ops")
                    for mb in range(2):
                        P.op("pe", lambda e, mb=mb, oi=oi, h=h, c2=c2, pi=pis[mb]: e.matmul(
                            ops_[oi][:], Vm[:, mb, h * 256 + c2 * 128:h * 256 + (c2 + 1) * 128], PT[pi][:],
                            start=(mb == 0), stop=(mb == 1)), (b_Vm, b_PT[pis[mb]]), (b_ops[oi],))
                    fin(ops_[oi][:], b_ops[oi], (dps[di][:], b_dps[di]), gi, 24 + 2 * h + c2, qg, True)
        P.emit()


def stage_C(nc, P, l, S, env):
    g = env
    SFX = "_C%d" % l
    NG = S // 512
    x_cur, x_nxt = g["x_cur"], g["x_nxt"]
    with ExitStack() as st:
        sbt = lambda name, shape, dt: st.enter_context(nc.sbuf_tensor(name + SFX, list(shape), dt))
        pst = lambda name, shape, dt: st.enter_context(nc.psum_tensor(name + SFX, list(shape), dt))
        OG = sbt("OG", [128, 32, 512], BF16)
        MX = sbt("MX", [128, 32, 512], BF16)
        wbf = [sbt("cwbf%d" % i, [128, 4096], BF16) for i in range(4)]
        RR = [sbt("RR%d" % i, [128, 3, 512], BF16) for i in range(2)]
        T = [sbt("T%d" % i, [128, 512], F32) for i in range(3)]
        xr = [sbt("xr%d" % i, [128, 256], F32) for i in range(3)]
        yo = [sbt("yo%d" % i, [128, 256], F32) for i in range(3)]
        ya = [pst("ya%d" % i, [128, 512], F32) for i in range(2)]
        yb = [pst("yb%d" % i, [128, 512], F32) for i in range(2)]
        ym = [pst("ym%d" % i, [128, 512], F32) for i in range(2)]
        op_ = [pst("op%d" % i, [128, 512], F32) for i in range(2)]
        B = lambda n: [Buf() for _ in range(n)]
        b_OG, b_MX = Buf(), Buf()
        b_wbf, b_RR, b_T, b_xr, b_yo = B(4), B(2), B(3), B(3), B(3)
        b_ya, b_yb, b_ym, b_op = B(2), B(2), B(2), B(2)
        rot = {}
        P.barrier()

        def wload(src, n):
            i = _rot(wbf, rot, "w")
            P.op("sp", lambda e: e.dma_start(out=wbf[i][:, :n], in_=src), (), (b_wbf[i],), dma=True)
            return wbf[i], b_wbf[i]

        for tg in range(NG):
            t0_, t1_ = tg * 512, (tg + 1) * 512
            for c0 in range(0, 32, 8):
                P.op("sp", lambda e, c0=c0, t0_=t0_, t1_=t1_: e.dma_start(
                    out=OG[:, c0:c0 + 8, :], in_=g["OGT"].ap()[c0:c0 + 8, :, t0_:t1_].rearrange("c p s -> p c s")),
                    (g["b_OG"][tg],), (b_OG,), dma=True)
            for cc in range(32):
                pa = _rot(ya, rot, "ya")
                for (wt, nk, k0, ps, bps) in ((g["wb"]["wPA"], 16, 0, ya, b_ya), (g["wb"]["wPB"], 8, 16, yb, b_yb),
                                              (g["wb"]["wPM"], 8, 24, ym, b_ym)):
                    w, bw = wload(wt[l].ap()[cc], nk * 128)
                    for kc in range(nk):
                        P.op("pe", lambda e, kc=kc, w=w, ps=ps, nk=nk, k0=k0, pa=pa: e.matmul(
                            ps[pa][:], w[:, kc * 128:(kc + 1) * 128], OG[:, k0 + kc, :], start=(kc == 0),
                            stop=(kc == nk - 1)), (bw, b_OG), (bps[pa],))
                ri = _rot(RR, rot, "RR")
                P.op("sp", lambda e, ri=ri, cc=cc, t0_=t0_, t1_=t1_: e.dma_start(
                    out=RR[ri][:], in_=g["RT"].ap()[cc:96:32][:, :, t0_:t1_].rearrange("c p s -> p c s")),
                    (g["b_A"][tg],), (b_RR[ri],), dma=True)
                P.op("dve", lambda e, ri=ri, pa=pa: e.tensor_tensor(out=T[0][:], in0=ya[pa][:], in1=RR[ri][:, 0, :],
                                                             op=ALU.mult), (b_ya[pa], b_RR[ri]), (b_T[0],))
                P.op("dve", lambda e, ri=ri, pa=pa: e.tensor_tensor(out=T[1][:], in0=yb[pa][:], in1=RR[ri][:, 1, :],
                                                             op=ALU.mult), (b_yb[pa], b_RR[ri]), (b_T[1],))
                P.op("dve", lambda e, ri=ri, pa=pa: e.tensor_tensor(out=T[2][:], in0=ym[pa][:], in1=RR[ri][:, 2, :],
                                                             op=ALU.mult), (b_ym[pa], b_RR[ri]), (b_T[2],))
                P.op("pool", lambda e: e.tensor_tensor(out=T[0][:], in0=T[0][:], in1=T[1][:], op=ALU.add),
                     (b_T[0], b_T[1]), (b_T[0],))
                P.op("pool", lambda e, cc=cc: e.tensor_tensor(out=MX[:, cc, :], in0=T[0][:], in1=T[2][:], op=ALU.add),
                     (b_T[0], b_T[2]), (b_MX,))
            for oc in range(16):
                ws = []
                for half in range(2):
                    ws.append(wload(g["wb"]["wOUT"][l].ap()[oc][:, half * 4096:(half + 1) * 4096], 4096))
                    if half == 0:
                        pass
                for tb in range(4):
                    r0 = t0_ + tb * 128
                    xi = _rot(xr, rot, "xr")
                    P.op("sp", lambda e, xi=xi, r0=r0, oc=oc: e.dma_start(
                        out=xr[xi][:], in_=x_cur.ap()[r0:r0 + 128, oc * 256:(oc + 1) * 256]),
                        (g["bx"](x_cur, tg),), (b_xr[xi],), dma=True)
                    oi = _rot(op_, rot, "op")
                    for kc in range(32):
                        w, bw = ws[kc // 16]
                        k2 = kc % 16
                        P.op("pe", lambda e, kc=kc, k2=k2, w=w, oi=oi, tb=tb: e.matmul(
                            op_[oi][:, :256], MX[:, kc, tb * 128:(tb + 1) * 128], w[:, k2 * 256:(k2 + 1) * 256],
                            start=(kc == 0), stop=(kc == 31)), (bw, b_MX), (b_op[oi],))
                    yi = _rot(yo, rot, "yo")
                    P.op("dve", lambda e, oi=oi, xi=xi, yi=yi: e.tensor_tensor(out=yo[yi][:], in0=op_[oi][:, :256],
                                                                              in1=xr[xi][:], op=ALU.add),
                         (b_op[oi], b_xr[xi]), (b_yo[yi],))
                    P.op("act", lambda e, yi=yi, r0=r0, oc=oc: e.dma_start(
                        out=x_nxt.ap()[r0:r0 + 128, oc * 256:(oc + 1) * 256], in_=yo[yi][:]),
                        (b_yo[yi],), (g["bx"](x_nxt, tg),), dma=True)
        P.emit()


def _tile(W, cw):
    K, N = W.shape
    return np.ascontiguousarray(W.reshape(K // 128, 128, N // cw, cw).transpose(2, 1, 0, 3)).reshape(
        N // cw, 128, (K // 128) * cw)


def _col(v):
    return np.ascontiguousarray(v.reshape(-1, 128).T)


_PERM = np.concatenate([np.arange(32, 64), np.arange(0, 32)])


def make_consts():
    c = np.zeros((128, NCONST), np.float32)
    p = np.arange(128)
    c[:, 0:128] = np.eye(128)
    c[:, 128:256] = 1.0
    c[:, 256:384] = -((p[:, None] >= p[None, :]).astype(np.float32))
    c[:, 384:512] = (p[:, None] % 64) == (p[None, :] % 64)
    k = p[:, None]
    i = p[None, :]
    dA = (k // 64) <= (i // 64)
    dB = k < i
    for j in range(4):
        for m in range(4):
            for (base, dm) in ((512, dA), (2560, dB)):
                blk = np.zeros((128, 128)) if m < j else (dm if m == j else np.ones((128, 128)))
                c[:, base + j * 512 + m * 128: base + j * 512 + (m + 1) * 128] = blk
    freq = (10000.0 ** (-(np.arange(32, dtype=np.float32)) / np.float32(32))).astype(np.float32)
    c[:, 4608] = freq[p % 32]
    ph = np.where(p < 64, 0.25, np.where(p < 96, 0.5, 0.0))
    c[:, 4609] = ph
    return c


def prep_weights(inp, DEPTH):
    out = {k: [] for k in ("gpack", "wF", "wVB", "wUQ", "wUK", "wUV", "wPA", "wPB", "wPM", "wOUT", "wMK", "wMV")}
    for l in range(DEPTH):
        w_in = np.asarray(inp["w_in"][l])
        kr = w_in[:, 1536:1600]
        colsF = np.concatenate([w_in[:, 0:1024], w_in[:, 1024:1536], kr, kr[:, _PERM], w_in[:, 1600:3648],
                                w_in[:, 6720:7744], w_in[:, 8768:9792], w_in[:, 3648:4672], w_in[:, 4672:5696],
                                w_in[:, 7744:8768], w_in[:, 9792:22080]], axis=1)
        out["wF"].append(_tile(colsF, 128))
        out["wVB"].append(_tile(w_in[:, 5696:6720], 256))
        w_uq = np.asarray(inp["w_uq"][l]).reshape(1024, 16, 192)
        nope = w_uq[:, :, 0:128].reshape(1024, 2048)
        rope = w_uq[:, :, 128:192]
        rope2 = np.concatenate([rope, rope[:, :, _PERM]], axis=2).reshape(1024, 2048)
        out["wUQ"].append(_tile(np.concatenate([nope, rope2], axis=1), 128))
        w_ukv = np.asarray(inp["w_ukv"][l]).reshape(512, 16, 256)
        out["wUK"].append(_tile(np.ascontiguousarray(w_ukv[:, :, 0:128]).reshape(512, 2048), 128))
        out["wUV"].append(_tile(np.ascontiguousarray(w_ukv[:, :, 128:256]).reshape(512, 2048), 256))
        out["wPA"].append(_tile(np.asarray(inp["w_pa"][l]), 128))
        out["wPB"].append(_tile(np.asarray(inp["w_pb"][l]), 128))
        out["wPM"].append(_tile(np.asarray(inp["w_pm"][l]), 128))
        out["wOUT"].append(_tile(np.asarray(inp["w_out"][l]), 256))
        w_mkv = np.asarray(inp["w_mkv"][l])
        out["wMK"].append(_tile(w_mkv[:, 0:1024], 128))
        out["wMV"].append(_tile(w_mkv[:, 1024:2048], 256))
        gq = np.asarray(inp["g_qn_a"][l])
        gk = np.asarray(inp["g_kn_a"][l])
        gpk = np.concatenate([
            _col(np.asarray(inp["g_pre"][l])), _col(np.asarray(inp["g_q_lat"][l])),
            _col(np.asarray(inp["g_kv_lat"][l])),
            gq[0:128, None], np.concatenate([gq[128:192], gq[128:192][_PERM]])[:, None],
            gk[0:128, None], np.concatenate([gk[128:192], gk[128:192][_PERM]])[:, None],
            _col(np.asarray(inp["g_mem"][l])), _col(np.asarray(inp["g_qn_m"][l])),
            _col(np.asarray(inp["g_kn_m"][l]))], axis=1).astype(np.float32)
        assert gpk.shape == (128, NG_COLS)
        out["gpack"].append(gpk)
    return {k: np.ascontiguousarray(np.stack(v)).astype(np.float32) for k, v in out.items()}


def run(inp, S, DEPTH, ncores, debug=(), stages="PABC", trace=False):
    nc = build(S, DEPTH, debug, stages)
    wd = prep_weights(inp, DEPTH)
    consts = make_consts()
    in_maps = []
    for c in range(ncores):
        m = dict(wd)
        m["x"] = np.ascontiguousarray(np.asarray(inp["x"][c], dtype=np.float32))
        m["mem"] = np.ascontiguousarray(np.asarray(inp["mem"][c], dtype=np.float32))
        m["pos"] = np.ascontiguousarray(np.asarray(inp["positions"][c], dtype=np.int32).reshape(1, S))
        m["consts"] = consts
        in_maps.append(m)
    res = run_bass_kernel_spmd(nc, in_maps, core_ids=list(range(ncores)), trace=trace)
    return res


def kernel(**inputs):
    S, DEPTH = 8192, 4
    res = run(inputs, S, DEPTH, 2)
    return np.stack([res.results[c]["y"] for c in range(2)]).astype(np.float32)
```
